# Optimizing a Trainium2 kernel written in Bass

```python
import math
import jax, jax.numpy as jnp
from jax import lax
import numpy as np

D_MODEL = 1024
BATCH = 4
SEQ = 8192
DEPTH = 2

N_MIXERS = 2
N_A_LAYERS = (DEPTH + 1) // 2
N_B_LAYERS = DEPTH // 2
DA_HEADS = 8
DA_HEAD_DIM = 64
DA_V_DIM = 2 * DA_HEAD_DIM
Q_BLOCK = 128
REL_BUCKETS = 32
REL_MAX_EXACT = REL_BUCKETS // 2
REL_MAX_DIST = 128
SGU_CHUNK = 128
SGU_WIDTH = D_MODEL
SGU_GROUPS = 8
SGU_GROUP_DIM = SGU_WIDTH // SGU_GROUPS
D_FF = 2816
N_MOD = 9
EPS = 1e-6
NEG_INF = -1e30

kernel_name = 'hybrid_diffattn_sgu_macaron'


def rms_norm(x, g):
    x32 = x.astype(jnp.float32)
    y = x32 * lax.rsqrt(jnp.mean(x32 * x32, axis=-1, keepdims=True) + EPS)
    return (y * g.astype(jnp.float32)).astype(x.dtype)


def layer_norm(x, g, b):
    x32 = x.astype(jnp.float32)
    mu = jnp.mean(x32, axis=-1, keepdims=True)
    xc = x32 - mu
    y = xc * lax.rsqrt(jnp.mean(xc * xc, axis=-1, keepdims=True) + EPS)
    return (y * g.astype(jnp.float32) + b.astype(jnp.float32)).astype(x.dtype)


def modulate(h, shift, scale):
    return h * (1.0 + scale[:, None, :]) + shift[:, None, :]


def swiglu(h, wg, wu, wd):
    return (jax.nn.silu(h @ wg) * (h @ wu)) @ wd


def t5_bucket(rel):
    n = jnp.maximum(rel, 0)
    nf = jnp.maximum(n, 1).astype(jnp.float32)
    large = REL_MAX_EXACT + (jnp.log(nf / REL_MAX_EXACT) / math.log(REL_MAX_DIST / REL_MAX_EXACT)
                             * (REL_BUCKETS - REL_MAX_EXACT)).astype(jnp.int32)
    large = jnp.minimum(large, REL_BUCKETS - 1)
    return jnp.where(n < REL_MAX_EXACT, n, large)


def diff_attention(h, w_qkv, q_g, k_g, lq1, lk1, lq2, lk2, subln_g, w_o, rel_bias, lambda_init):
    B, S, _ = h.shape
    q, k, v = jnp.split(h @ w_qkv, 3, axis=-1)
    q = rms_norm(q.reshape(B, S, DA_HEADS, 2, DA_HEAD_DIM), q_g)
    k = rms_norm(k.reshape(B, S, DA_HEADS, 2, DA_HEAD_DIM), k_g)
    v32 = v.reshape(B, S, DA_HEADS, DA_V_DIM).astype(jnp.float32)
    k32 = k.astype(jnp.float32)
    f32 = jnp.float32
    lam = (jnp.exp(jnp.sum(lq1.astype(f32) * lk1.astype(f32)))
           - jnp.exp(jnp.sum(lq2.astype(f32) * lk2.astype(f32))) + lambda_init)
    scale = 1.0 / math.sqrt(DA_HEAD_DIM)
    nb = S // Q_BLOCK
    qb = q.reshape(B, nb, Q_BLOCK, DA_HEADS, 2, DA_HEAD_DIM).transpose(1, 0, 2, 3, 4, 5)
    k_pos = jnp.arange(S, dtype=jnp.int32)
    table = rel_bias.astype(f32)

    def block(args):
        q_blk, bi = args
        q_pos = bi * Q_BLOCK + jnp.arange(Q_BLOCK, dtype=jnp.int32)
        rel = q_pos[:, None] - k_pos[None, :]
        bias = jnp.where((rel >= 0)[..., None], table[t5_bucket(rel)], NEG_INF)
        bias = bias.transpose(2, 0, 1)
        logits = jnp.einsum('bqhcd,bkhcd->bhcqk', q_blk.astype(f32), k32) * scale + bias[None, :, None]
        p = jax.nn.softmax(logits, axis=-1)
        attn = p[:, :, 0] - lam * p[:, :, 1]
        o = jnp.einsum('bhqk,bkhe->bqhe', attn, v32)
        o = rms_norm(o, subln_g) * (1.0 - lambda_init)
        return o.reshape(B, Q_BLOCK, DA_HEADS * DA_V_DIM).astype(h.dtype)

    out = lax.map(block, (qb, jnp.arange(nb, dtype=jnp.int32)))
    out = out.transpose(1, 0, 2, 3).reshape(B, S, DA_HEADS * DA_V_DIM)
    return out @ w_o


def spatial_gating(h, w_in, ln_g, ln_b, w_s, b_s, w_out):
    B, S, _ = h.shape
    z = jax.nn.gelu(h @ w_in, approximate=False)
    u, v = jnp.split(z, 2, axis=-1)
    v = layer_norm(v, ln_g, ln_b)
    nc = S // SGU_CHUNK
    v = v.reshape(B, nc, SGU_CHUNK, SGU_GROUPS, SGU_GROUP_DIM)
    causal = jnp.tril(jnp.ones((SGU_CHUNK, SGU_CHUNK), dtype=w_s.dtype))
    mixed = jnp.einsum('gts,bnsgc->bntgc', w_s * causal, v) + b_s.T[:, :, None]
    gated = u * mixed.reshape(B, S, SGU_WIDTH)
    return gated @ w_out


def _normal(k, shape, std):
    return jax.random.normal(k, shape, jnp.float32) * std


def setup_inputs(seed: int = 0) -> dict:
    key = jax.random.key(seed)
    ks = jax.random.split(key, 32)
    D, F = D_MODEL, D_FF
    gain = lambda k, shape: 1.0 + _normal(k, shape, 0.1)
    return {
        'x': _normal(ks[0], (BATCH, SEQ, D), 1.0),
        'c': _normal(ks[1], (BATCH, D), 1.0),
        'rel_bias': _normal(ks[2], (REL_BUCKETS, DA_HEADS), 0.5),
        'ada_w': _normal(ks[3], (DEPTH, D, N_MOD * D), 0.5 * D ** -0.5),
        'ada_b': _normal(ks[4], (DEPTH, N_MOD * D), 0.02),
        'ln_ffn1': gain(ks[5], (DEPTH, D)),
        'ffn1_wg': _normal(ks[6], (DEPTH, D, F), D ** -0.5),
        'ffn1_wu': _normal(ks[7], (DEPTH, D, F), D ** -0.5),
        'ffn1_wd': _normal(ks[8], (DEPTH, F, D), F ** -0.5),
        'ln_mix': gain(ks[9], (DEPTH, D)),
        'ln_ffn2': gain(ks[10], (DEPTH, D)),
        'ffn2_wg': _normal(ks[11], (DEPTH, D, F), D ** -0.5),
        'ffn2_wu': _normal(ks[12], (DEPTH, D, F), D ** -0.5),
        'ffn2_wd': _normal(ks[13], (DEPTH, F, D), F ** -0.5),
        'ln_out': gain(ks[14], (DEPTH, D)),
        'attn_w_qkv': _normal(ks[15], (N_A_LAYERS, D, 3 * D), D ** -0.5),
        'attn_q_norm': gain(ks[16], (N_A_LAYERS, DA_HEAD_DIM)),
        'attn_k_norm': gain(ks[17], (N_A_LAYERS, DA_HEAD_DIM)),
        'attn_lq1': _normal(ks[18], (N_A_LAYERS, DA_HEAD_DIM), 0.1),
        'attn_lk1': _normal(ks[19], (N_A_LAYERS, DA_HEAD_DIM), 0.1),
        'attn_lq2': _normal(ks[20], (N_A_LAYERS, DA_HEAD_DIM), 0.1),
        'attn_lk2': _normal(ks[21], (N_A_LAYERS, DA_HEAD_DIM), 0.1),
        'attn_subln': gain(ks[22], (N_A_LAYERS, DA_V_DIM)),
        'attn_w_o': _normal(ks[23], (N_A_LAYERS, D, D), D ** -0.5),
        'sgu_w_in': _normal(ks[24], (N_B_LAYERS, D, 2 * SGU_WIDTH), D ** -0.5),
        'sgu_ln_g': gain(ks[25], (N_B_LAYERS, SGU_WIDTH)),
        'sgu_ln_b': _normal(ks[26], (N_B_LAYERS, SGU_WIDTH), 0.02),
        'sgu_w_s': _normal(ks[27], (N_B_LAYERS, SGU_GROUPS, SGU_CHUNK, SGU_CHUNK), SGU_CHUNK ** -0.5),
        'sgu_b_s': gain(ks[28], (N_B_LAYERS, SGU_GROUPS, SGU_CHUNK)),
        'sgu_w_out': _normal(ks[29], (N_B_LAYERS, SGU_WIDTH, D), SGU_WIDTH ** -0.5),
    }


def reference(x, c, rel_bias, ada_w, ada_b, ln_ffn1, ffn1_wg, ffn1_wu, ffn1_wd, ln_mix,
              ln_ffn2, ffn2_wg, ffn2_wu, ffn2_wd, ln_out, attn_w_qkv, attn_q_norm, attn_k_norm,
              attn_lq1, attn_lk1, attn_lq2, attn_lk2, attn_subln, attn_w_o, sgu_w_in, sgu_ln_g,
              sgu_ln_b, sgu_w_s, sgu_b_s, sgu_w_out):
    c_act = jax.nn.silu(c)
    for i in range(DEPTH):
        mod = c_act @ ada_w[i] + ada_b[i]
        sh1, sc1, g1, sh2, sc2, g2, sh3, sc3, g3 = jnp.split(mod, N_MOD, axis=-1)
        h = modulate(rms_norm(x, ln_ffn1[i]), sh1, sc1)
        x = x + 0.5 * g1[:, None, :] * swiglu(h, ffn1_wg[i], ffn1_wu[i], ffn1_wd[i])
        h = modulate(rms_norm(x, ln_mix[i]), sh2, sc2)
        j = i // N_MIXERS
        if i % N_MIXERS == 0:
            lambda_init = 0.8 - 0.6 * math.exp(-0.3 * i)
            mix = diff_attention(h, attn_w_qkv[j], attn_q_norm[j], attn_k_norm[j], attn_lq1[j],
                                 attn_lk1[j], attn_lq2[j], attn_lk2[j], attn_subln[j], attn_w_o[j],
                                 rel_bias, lambda_init)
        else:
            mix = spatial_gating(h, sgu_w_in[j], sgu_ln_g[j], sgu_ln_b[j], sgu_w_s[j], sgu_b_s[j],
                                 sgu_w_out[j])
        x = x + g2[:, None, :] * mix
        h = modulate(rms_norm(x, ln_ffn2[i]), sh3, sc3)
        x = x + 0.5 * g3[:, None, :] * swiglu(h, ffn2_wg[i], ffn2_wu[i], ffn2_wd[i])
        x = rms_norm(x, ln_out[i])
    return x
```

```python
import math
from contextlib import ExitStack

import numpy as np
import concourse.bass as bass
import concourse.mybir as mybir
from concourse.bass_utils import run_bass_kernel_spmd

F32 = mybir.dt.float32
BF16 = mybir.dt.bfloat16
AF = mybir.ActivationFunctionType
ALU = mybir.AluOpType
AX = mybir.AxisListType

D = 1024
NKC = 8
FF = 2816
NF = 22
H = 8
EPS = 1e-6
COMPUTE = ("pe", "act", "dve", "pool")
SLOT_B = 5632
DEPTH = 5


class Reg:
    __slots__ = ("name", "w", "r", "cnt")

    def __init__(self, name):
        self.name = name
        self.w = None
        self.r = []
        self.cnt = 0


class RegDict(dict):
    def __missing__(self, k):
        v = Reg(str(k))
        self[k] = v
        return v


class Prog:
    def __init__(self, dry=False):
        self.dry = dry
        self.streams = {e: [] for e in ("pe", "act", "dve", "pool", "sp")}
        self.count = {e: 0 for e in COMPUTE}
        self.seen = {e: {} for e in self.streams}
        self.anchors = []

    def _need(self, eng, waits, ev, raw):
        key, val = ev
        if key == eng and not raw:
            return
        if self.seen[eng].get(key, 0) >= val:
            return
        if waits.get(key, 0) < val:
            waits[key] = val

    def _deps(self, eng, reads, writes, anchor=None):
        waits = {}
        for R in reads:
            if R.w is not None:
                self._need(eng, waits, R.w, True)
        for R in writes:
            if R.w is not None:
                if not (anchor is not None and R.w[0] is anchor and not R.r):
                    self._need(eng, waits, R.w, False)
            for ev in R.r:
                self._need(eng, waits, ev, False)
        for k, v in waits.items():
            self.seen[eng][k] = v
        return waits

    def _mark(self, ev, reads, writes):
        for R in writes:
            R.w = ev
            R.r = []
        for R in reads:
            if R not in writes:
                R.r.append(ev)

    def op(self, eng, fn, reads=(), writes=()):
        if self.dry:
            return None
        waits = self._deps(eng, reads, writes)
        self.count[eng] += 1
        ev = (eng, self.count[eng])
        self.streams[eng].append(("op", fn, waits, None))
        self._mark(ev, reads, writes)
        return ev

    def dma(self, queue, out, in_, anchor, reads=(), writes=()):
        if self.dry:
            return None
        waits = self._deps(queue, reads, writes, anchor)
        if anchor.cnt == 0:
            self.anchors.append(anchor)
        anchor.cnt += 16
        ev = (anchor, anchor.cnt)
        self.streams[queue].append(("dma", (out, in_), waits, anchor))
        self._mark(ev, reads, writes)
        return ev

    def wait(self, eng, events):
        if self.dry:
            return
        waits = {}
        for ev in events:
            if ev is not None:
                self._need(eng, waits, ev, True)
        for k, v in waits.items():
            self.seen[eng][k] = v
        if waits:
            self.streams[eng].append(("wait", None, waits, None))

    def barrier(self):
        if self.dry:
            return
        evs = [(e, self.count[e]) for e in COMPUTE if self.count[e] > 0 and e != "pool"]
        evs += [(a, a.cnt) for a in self.anchors if not a.name.startswith("PREP")]
        for e in self.streams:
            if e != "pool":
                self.wait(e, evs)

    def emit(self, nc):
        with ExitStack() as es:
            sems = {}
            for e in COMPUTE:
                sems[e] = es.enter_context(nc.semaphore("s_" + e))
            for i, a in enumerate(self.anchors):
                sems[a] = es.enter_context(nc.semaphore("d%d" % i))
            block = es.enter_context(nc.Block())

            def run(engname):
                def body(e):
                    for kind, fn, waits, anchor in self.streams[engname]:
                        for k, v in waits.items():
                            e.wait_ge(sems[k], v)
                        if kind == "op":
                            fn(e).then_inc(sems[engname], 1)
                        elif kind == "dma":
                            e.dma_start(out=fn[0], in_=fn[1]).then_inc(sems[anchor], 16)
                return body

            block.tensor(run("pe"))
            block.scalar(run("act"))
            block.vector(run("dve"))
            block.gpsimd(run("pool"))
            block.sync(run("sp"))


class WStream:
    def __init__(self):
        self.keys = []
        self.collect = True
        self.ptr = 0
        self.loaded = 0
        self.loader = None

    def use(self, key):
        if self.collect:
            self.keys.append(key)
            return 0
        i = self.ptr
        assert self.keys[i] == key, (self.keys[i], key)
        self.ptr += 1
        while self.loaded < min(len(self.keys), i + DEPTH):
            self.loader(self.keys[self.loaded], self.loaded % DEPTH)
            self.loaded += 1
        return i % DEPTH


def build(S, debug=False):
    NB = S // 128
    NOWN = NB // 2
    NT1 = S // 1024
    NT2 = S // 2048
    NG = NOWN // 4
    nc = bass.Bass("TRN2", target_bir_lowering=False)

    def din(name, shape, dt=F32):
        return nc.dram_tensor(name, shape, dt, kind="ExternalInput").ap()

    def dscr(name, shape, dt):
        return nc.dram_tensor(name, shape, dt, kind="ExternalOutput" if debug else "Internal").ap()

    xall = din("xall", [S, D])
    ccol = din("ccol", [128, 8])
    ada_w = din("ada_w", [2, D, 9 * D])
    ada_b2 = din("ada_b2", [128, 2, 72])
    lnv = din("lnv", [128, 2, 32])
    wg = [din("wg1", [2, D, FF]), din("wg2", [2, D, FF])]
    wu = [din("wu1", [2, D, FF]), din("wu2", [2, D, FF])]
    wd = [din("wd1", [2, FF, D]), din("wd2", [2, FF, D])]
    wqkv = din("wqkv", [D, 3 * D])
    wo = din("wo", [D, D])
    w_in = din("w_in", [D, 2 * D])
    w_out = din("w_out", [D, D])
    avec = din("avec", [128, 4])
    lrow = din("lrow", [1, 256])
    sgv = din("sgv", [128, 16])
    w_s = din("w_s", [8, 128, 128])
    bsrow = din("bsrow", [1, 1024])
    relb = din("relb", [32, 8])
    ident = din("ident", [128, 128])
    tril = din("tril", [128, 128])
    ehot = din("ehot", [33, 768])
    jflip = din("jflip", [128, 128])
    bd64 = din("bd64", [128, 128])
    y = nc.dram_tensor("y", [NOWN * 128, D], F32, kind="ExternalOutput").ap()

    GU = [[dscr("GU%d%d" % (l, w), [NF, 128, 2, 1024], BF16) for w in range(2)] for l in range(2)]
    DD = [[dscr("DD%d%d" % (l, w), [8, 128, NF * 128], BF16) for w in range(2)] for l in range(2)]
    LH = {"qkv": dscr("LHqkv", [24, 128, 1024], BF16), "wo": dscr("LHwo", [8, 128, 1024], BF16),
          "win": dscr("LHwin", [16, 128, 1024], BF16), "wout": dscr("LHwout", [8, 128, 1024], BF16)}
    KT = dscr("KT", [H, 128, NB * 128], BF16)
    VV = dscr("VV", [H, 128, NB, 128], BF16)
    QT = dscr("QT", [H, 128, NOWN * 128], BF16)
    X1 = dscr("X1", [NT1, 128, 8, 512], F32)
    AO = dscr("AO", [H, 128, NOWN * 128], BF16)
    GD = dscr("GD", [8, 768], F32)
    DBG = dscr("DBG", [128, 8, 1024], F32) if debug else None

    es = ExitStack()

    def sbt(name, shape, dt):
        return es.enter_context(nc.sbuf_tensor(name, shape, dt))

    IDF = sbt("IDF", [128, 128], F32)
    IDB = sbt("IDB", [128, 128], BF16)
    ONESB = sbt("ONESB", [128, 128], BF16)
    BD64 = sbt("BD64", [128, 128], BF16)
    ONESF = sbt("ONESF", [128, 128], F32)
    TRIL = sbt("TRIL", [128, 128], F32)
    JF = sbt("JF", [128, 128], F32)
    CCOL = sbt("CCOL", [128, 8], F32)
    CACT = sbt("CACT", [128, 8], F32)
    MODV = sbt("MODV", [128, 2, 72], F32)
    ADAB = sbt("ADAB", [128, 2, 72], F32)
    LNV = sbt("LNV", [128, 2, 32], F32)
    GM = sbt("GM", [128, 2, 24], F32)
    SHB = sbt("SHB", [128, 2, 24], BF16)
    HG = sbt("HG", [128, 2, 24], F32)
    CGU = sbt("CGU", [128, 4, 44], F32)
    CQK = sbt("CQK", [128, 16], F32)
    CVB = sbt("CVB", [128, 1024], F32)
    CZ = sbt("CZ", [128, 16], F32)
    AVEC = sbt("AVEC", [128, 4], F32)
    LAMV = sbt("LAMV", [128, 8], F32)
    LROW = sbt("LROW", [128, 256], F32)
    LTMP = sbt("LTMP", [128, 128], F32)
    SGV = sbt("SGV", [128, 16], F32)
    WMT = sbt("WMT", [128, 8, 128], BF16)
    B2 = sbt("B2", [128, 8, 128], F32)
    B31 = sbt("B31", [128, 8], F32)
    BSROW = sbt("BSROW", [128, 1024], F32)
    PF = sbt("PF", [128, 2, 2048], F32)
    PBF = sbt("PBF", [128, 2, 2048], BF16)
    CVROW = BSROW
    RINGT = sbt("RINGT", [128, DEPTH, SLOT_B // 2], BF16)
    ARW = 33500
    ARENA = sbt("ARENA", [128, ARW], F32)
    PS = [es.enter_context(nc.psum_tensor("PS%d" % i, [128, 512], F32)) for i in range(8)]

    class Arena:
        def __init__(self):
            self.off = 0

        def reset(self):
            self.off = 0

        def f32(self, n):
            a = ARENA[:, self.off:self.off + n]
            self.off += n
            assert self.off <= ARW, self.off
            return a

        def bf16(self, n):
            assert n % 2 == 0
            a = ARENA[:, self.off:self.off + n // 2].bitcast(BF16)
            self.off += n // 2
            assert self.off <= ARW, self.off
            return a

    AR = Arena()

    def emit_all(P, ws):
        rg = RegDict()
        psr = [rg["PS%d" % i] for i in range(8)]

        def MM(out, lhsT, rhs, start, stop, rd, wr):
            P.op("pe", lambda e: e.matmul(out, lhsT=lhsT, rhs=rhs, start=start, stop=stop), rd, wr)

        def TR(out, in_, idn, rd, wr):
            P.op("pe", lambda e: e.transpose(out=out, in_=in_, identity=idn), rd, wr)

        def ACT(out, in_, func, rd, wr, bias=None, scale=None):
            kw = {}
            if bias is not None:
                kw["bias"] = bias
            if scale is not None:
                kw["scale"] = scale
            P.op("act", lambda e: e.activation(out=out, in_=in_, func=func, **kw), rd, wr)

        def STT(out, in0, scalar, in1, op0, op1, rd, wr):
            P.op("dve", lambda e: e.scalar_tensor_tensor(out=out, in0=in0, scalar=scalar, in1=in1, op0=op0, op1=op1), rd, wr)

        def TT(out, in0, in1, op, rd, wr, eng="dve"):
            P.op(eng, lambda e: e.tensor_tensor(out=out, in0=in0, in1=in1, op=op), rd, wr)

        def TS(out, in0, s1, s2, op0, op1, rd, wr, eng="dve"):
            if s2 is None:
                P.op(eng, lambda e: e.tensor_scalar(out=out, in0=in0, scalar1=s1, scalar2=None, op0=op0), rd, wr)
            else:
                P.op(eng, lambda e: e.tensor_scalar(out=out, in0=in0, scalar1=s1, scalar2=s2, op0=op0, op1=op1), rd, wr)

        def CP(eng, out, in_, rd, wr):
            if eng == "act":
                P.op("act", lambda e: e.activation(out=out, in_=in_, func=AF.Copy), rd, wr)
            else:
                P.op(eng, lambda e: e.tensor_copy(out=out, in_=in_), rd, wr)

        def RSTD_ACT(out, tmp, tmp_r, in_, scale, rd, wr):
            ACT(tmp, in_, AF.Ln, rd, [tmp_r], bias=EPSB[:, 0:1], scale=scale)
            ACT(out, tmp, AF.Exp, [tmp_r], wr, scale=-0.5)

        def RECIP(out, in_, rd, wr):
            P.op("dve", lambda e: e.reciprocal(out=out, in_=in_), rd, wr)

        def MEMSET(eng, ap, val, wr):
            P.op(eng, lambda e: e.memset(ap, val), (), wr)

        def DMA(q, out, in_, anchor, rd, wr):
            return P.dma(q, out, in_, anchor, rd, wr)

        for name, dst, src in (("IDF", IDF, ident), ("TRIL", TRIL, tril), ("JF", JF, jflip), ("CCOL", CCOL, ccol),
                               ("ADAB", ADAB, ada_b2), ("LNV", LNV, lnv), ("AVEC", AVEC, avec), ("SGV", SGV, sgv)):
            DMA("sp", dst[:], src, rg[name], [], [rg[name]])
        DMA("sp", LROW[:, :], lrow.partition_broadcast(128), rg["LROW"], [], [rg["LROW"]])
        DMA("sp", B31[:, :], relb[31:32, :].partition_broadcast(128), rg["B31"], [], [rg["B31"]])
        DMA("sp", BSROW[0:1, :], bsrow, rg["BSROW"], [], [rg["BSROW"]])
        AR.reset()
        bdf = AR.f32(128)
        DMA("sp", bdf, bd64, rg["bdf"], [], [rg["bdf"]])
        CP("dve", BD64[:, :], bdf, [rg["bdf"]], [rg["BD64"]])
        CP("dve", IDB[:, :], IDF[:, :], [rg["IDF"]], [rg["IDB"]])
        MEMSET("dve", ONESB[:, :], 1.0, [rg["ONESB"]])
        MEMSET("dve", ONESF[:, :], 1.0, [rg["ONESF"]])
        ACT(CACT[:, :], CCOL[:, :], AF.Silu, [rg["CCOL"]], [rg["CACT"]])
        for t in range(2):
            TT(LTMP[:, 0:64], LROW[:, 128 * t:128 * t + 64], LROW[:, 128 * t + 64:128 * t + 128], ALU.mult,
               [rg["LROW"]], [rg["LTMP"]])
            P.op("dve", lambda e, t=t: e.reduce_sum(out=LAMV[:, t:t + 1], in_=LTMP[:, 0:64], axis=AX.X),
                 [rg["LTMP"]], [rg["LAMV"]])
        ACT(LAMV[:, 2:4], LAMV[:, 0:2], AF.Exp, [rg["LAMV"]], [rg["LAMV"]])
        TT(LAMV[:, 4:5], LAMV[:, 3:4], LAMV[:, 2:3], ALU.subtract, [rg["LAMV"]], [rg["LAMV"]])
        TS(LAMV[:, 4:5], LAMV[:, 4:5], -0.2, None, ALU.add, None, [rg["LAMV"]], [rg["LAMV"]])
        TS(LAMV[:, 5:6], AVEC[:, 2:3], 0.8, None, ALU.mult, None, [rg["AVEC"], rg["LAMV"]], [rg["LAMV"]])
        NLAM = LAMV[:, 4:5]
        GSUB = LAMV[:, 5:6]

        ada_st = [AR.f32(2048) for _ in range(2)]
        for l in range(2):
            for jg in range(36):
                st = ada_st[jg % 2]
                sr = rg["ada_st%d" % (jg % 2)]
                DMA("sp", st.rearrange("p (k j) -> p k j", k=8),
                    ada_w[l, :, jg * 256:(jg + 1) * 256].rearrange("(k p) j -> p k j", p=128), sr, [], [sr])
                for jj in range(2):
                    col = jg * 2 + jj
                    for kc in range(8):
                        MM(PS[7][:, col:col + 1], st[:, kc * 256 + jj * 128:kc * 256 + jj * 128 + 128], CACT[:, kc:kc + 1],
                           kc == 0, kc == 7, [sr, rg["CACT"]], [psr[7]])
            TT(MODV[:, l, :], PS[7][:, 0:72], ADAB[:, l, :], ALU.add, [psr[7], rg["ADAB"]], [rg["MODV"]])
            for j in range(3):
                STT(GM[:, l, 8 * j:8 * j + 8], MODV[:, l, (3 * j + 1) * 8:(3 * j + 2) * 8], 1.0, LNV[:, l, 8 * j:8 * j + 8],
                    ALU.add, ALU.mult, [rg["MODV"], rg["LNV"]], [rg["GM"]])
                CP("dve", SHB[:, l, 8 * j:8 * j + 8], MODV[:, l, 3 * j * 8:3 * j * 8 + 8], [rg["MODV"]], [rg["SHB"]])
                TS(HG[:, l, 8 * j:8 * j + 8], MODV[:, l, (3 * j + 2) * 8:(3 * j + 3) * 8], 1.0 if j == 1 else 0.5, None,
                   ALU.mult, None, [rg["MODV"]], [rg["HG"]])

        wsf = AR.f32(1024)
        wsv = wsf.rearrange("p (g s) -> p g s", g=8)
        DMA("sp", wsv, w_s.rearrange("g t s -> t g s"), rg["wsf"], [], [rg["wsf"]])
        TT(wsv, wsv, TRIL[:, :].unsqueeze(1).to_broadcast([128, 8, 128]), ALU.mult, [rg["wsf"], rg["TRIL"]], [rg["wsf"]])
        wmtf = AR.f32(1024)
        for hb in range(2):
            for g4 in range(4):
                g = hb * 4 + g4
                TR(PS[6][:, g4 * 128:(g4 + 1) * 128], wsf[:, g * 128:(g + 1) * 128], IDF[:, :], [rg["wsf"], rg["IDF"]], [psr[6]])
            CP("dve", wmtf[:, hb * 512:(hb + 1) * 512], PS[6][:, :], [psr[6]], [rg["wmtf"]])
        CP("dve", WMT[:, :, :], wmtf.rearrange("p (g t) -> p g t", g=8), [rg["wmtf"]], [rg["WMT"]])
        for hb in range(2):
            MM(PS[6][:, :], ONESF[:, :], wmtf[:, hb * 512:(hb + 1) * 512], True, True, [rg["ONESF"], rg["wmtf"]], [psr[6]])
            MM(PS[5][:, :], ONESF[0:1, :], BSROW[0:1, hb * 512:(hb + 1) * 512], True, True, [rg["ONESF"], rg["BSROW"]], [psr[5]])
            for g4 in range(4):
                g = hb * 4 + g4
                CP("act", LTMP[:, :], PS[5][:, g4 * 128:(g4 + 1) * 128], [psr[5]], [rg["LTMP"]])
                STT(B2[:, g, :], PS[6][:, g4 * 128:(g4 + 1) * 128], SGV[:, 8 + g:9 + g], LTMP[:, :], ALU.mult, ALU.add,
                    [psr[6], rg["SGV"], rg["LTMP"]], [rg["B2"]])

        pcount = [0]

        def prep_cols(src2d, C, dst, rname, only=None):
            for cg in (range(C // 256) if only is None else (only,)):
                b = pcount[0] % 2
                pcount[0] += 1
                rf, rb = rg["PREPF%d" % b], rg["PREPB%d" % b]
                rdst = rg[(rname, cg)]
                DMA("pool", PF[:, b, :].rearrange("p (k j) -> p k j", k=8),
                    src2d[:, cg * 256:(cg + 1) * 256].rearrange("(k p) j -> p k j", p=128), rf, [], [rf])
                for kk in range(8):
                    P.op("pool", lambda e, b=b, kk=kk: e.tensor_copy(
                        out=PBF[:, b, :].rearrange("p (s k j) -> p s k j", s=2, k=8)[:, :, kk, :],
                        in_=PF[:, b, :].rearrange("p (k s j) -> p s k j", k=8, s=2)[:, :, kk, :]), [rf], [rb])
                DMA("pool", dst[cg * 2:cg * 2 + 2].rearrange("s p n -> p s n"),
                    PBF[:, b, :].rearrange("p (s n) -> p s n", s=2), rb, [rb], [rdst])

        def prep_rows(src2d, dst, rname):
            for fg in range(11):
                b = pcount[0] % 2
                pcount[0] += 1
                rf, rb = rg["PREPF%d" % b], rg["PREPB%d" % b]
                rdst = rg[(rname, fg)]
                DMA("pool", PF[:, b, :].rearrange("p (f n) -> p f n", f=2),
                    src2d[fg * 256:(fg + 1) * 256, :].rearrange("(f p) n -> p f n", p=128), rf, [], [rf])
                for dd_ in range(8):
                    P.op("pool", lambda e, b=b, dd_=dd_: e.tensor_copy(
                        out=PBF[:, b, :].rearrange("p (d f j) -> p d f j", d=8, f=2)[:, dd_, :, :],
                        in_=PF[:, b, :].rearrange("p (f d j) -> p d f j", f=2, d=8)[:, dd_, :, :]), [rf], [rb])
                DMA("pool", dst[:, :, fg * 256:(fg + 1) * 256].rearrange("d p n -> p d n"),
                    PBF[:, b, :].rearrange("p (d n) -> p d n", d=8), rb, [rb], [rdst])

        def prep_ffn(l, w):
            for cg in range(FF // 256):
                for t, srcw in enumerate((wg[w], wu[w])):
                    prep_cols(srcw[l], FF, GU[l][w][:, :, t, :], "GU%d%d_%d" % (l, w, t), only=cg)
            prep_rows(wd[w][l], DD[l][w], "DD%d%d" % (l, w))

        prep_ffn(0, 0)
        prep_cols(wqkv, 3 * D, LH["qkv"], "LHqkv")
        prep_cols(wo, D, LH["wo"], "LHwo")
        prep_ffn(0, 1)
        prep_ffn(1, 0)
        prep_cols(w_in, 2 * D, LH["win"], "LHwin")
        prep_cols(w_out, D, LH["wout"], "LHwout")
        prep_ffn(1, 1)

        P.barrier()
        AR.reset()
        XFa = AR.f32(8192)
        XF = XFa.rearrange("p (c t) -> p c t", c=8)
        HBa = AR.bf16(8192)
        HB = HBa.rearrange("p (c t) -> p c t", c=8)
        ACTa = AR.bf16(NF * 1024)
        ACTB = ACTa.rearrange("p (f t) -> p f t", f=NF)
        RING = [RINGT[:, i, :] for i in range(DEPTH)]
        SQ = AR.bf16(4096).rearrange("p (c t) -> p c t", c=8)
        RS = AR.f32(512)
        RSTD = [AR.f32(512) for _ in range(2)]
        vc_off = AR.off
        KF = [AR.f32(512) for _ in range(2)]
        GS = [AR.bf16(512) for _ in range(2)]
        KN = [AR.bf16(512) for _ in range(2)]
        XIN = [AR.f32(1024) for _ in range(2)]
        YT = XIN
        assert AR.off - vc_off == 4096
        XIN = XIN + [AR.f32(1024) for _ in range(2)]
        YT = XIN
        VC = ARENA[:, vc_off:vc_off + 4096].rearrange("p (c t) -> p c t", c=8)
        ring_r = [rg["RING%d" % s] for s in range(DEPTH)]

        def xr(c, hf):
            return rg["XF%d_%d" % (c, hf)]

        def hr(c, hf):
            return rg["HB%d_%d" % (c, hf)]

        def ar_(f, hf):
            return rg["ACT%d_%d" % (f, hf)]

        def loader(key, s):
            kind = key[0]
            if kind == "gu":
                _, l, w, f = key
                DMA("sp", RING[s][:, 0:2048].rearrange("p (t n) -> p t n", t=2), GU[l][w][f], ring_r[s],
                    [rg[("GU%d%d_%d" % (l, w, t), f // 2)] for t in range(2)], [ring_r[s]])
            elif kind == "dd":
                _, l, w, d = key
                DMA("sp", RING[s][:, 0:NF * 128], DD[l][w][d], ring_r[s], [rg[("DD%d%d" % (l, w), fg)] for fg in range(11)], [ring_r[s]])
            elif kind == "lh":
                _, nm, j = key
                DMA("sp", RING[s][:, 0:1024], LH[nm][j], ring_r[s], [rg[("LH" + nm, j // 2)]], [ring_r[s]])
            elif kind == "v":
                _, ct = key
                DMA("sp", RING[s][:, 0:2048].rearrange("p (t n) -> p t n", t=2),
                    LH["qkv"][16 + 2 * ct:18 + 2 * ct].rearrange("s p n -> p s n"), ring_r[s], [rg[("LHqkv", 8 + ct)]], [ring_r[s]])

        ws.loader = loader
        gu_cnt = [0]
        y_cnt = [0]

        def emit_norm(gain_ap, dst, dst_r):
            for hf in range(2):
                cs = slice(hf * 512, (hf + 1) * 512)
                xrs = [xr(c, hf) for c in range(8)]
                ACT(SQ[:, :, :], XF[:, :, cs], AF.Square, xrs, [rg["SQ"]])
                for c in range(8):
                    MM(PS[6][:, :], ONESB[:, :], SQ[:, c, :], c == 0, c == 7, [rg["ONESB"], rg["SQ"]], [psr[6]])
                RSTD_ACT(RSTD[hf], RS, rg["RS"], PS[6][:, :], 1.0 / D, [psr[6]], [rg["RSTD%d" % hf]])
                for c in range(8):
                    STT(dst[:, c, cs], XF[:, c, cs], gain_ap[:, c:c + 1], RSTD[hf], ALU.mult, ALU.mult,
                        [xr(c, hf), rg["RSTD%d" % hf], rg["GM"], rg["LNV"]], [dst_r(c, hf)])

        def emit_ffn(l, w, first):
            j = 0 if w == 0 else 2
            emit_norm(GM[:, l, 8 * j:8 * j + 8], HB, hr)
            cgu = CGU[:, l * 2 + w, :]
            rcg = rg["CGU%d%d" % (l, w)]
            for f in range(NF):
                s = ws.use(("gu", l, w, f))
                slot = RING[s]
                if first:
                    for t in range(2):
                        for kc in range(8):
                            MM(PS[7][:, 2 * f + t:2 * f + t + 1], slot[:, t * 1024 + kc * 128:t * 1024 + kc * 128 + 128],
                               SHB[:, l, 8 * j + kc:8 * j + kc + 1], kc == 0, kc == 7, [ring_r[s], rg["SHB"]], [psr[7]])
                    CP("dve", cgu[:, 2 * f:2 * f + 2], PS[7][:, 2 * f:2 * f + 2], [psr[7]], [rcg])
                for hf in range(2):
                    cs = slice(hf * 512, (hf + 1) * 512)
                    k = gu_cnt[0] % 2
                    gu_cnt[0] += 1
                    pg, pu = PS[2 * k], PS[2 * k + 1]
                    for t, pp in ((0, pg), (1, pu)):
                        for kc in range(8):
                            MM(pp[:, :], slot[:, t * 1024 + kc * 128:t * 1024 + kc * 128 + 128], HB[:, kc, cs], kc == 0, kc == 7,
                               [ring_r[s], hr(kc, hf)], [psr[2 * k + t]])
                    ACT(GS[k], pg[:, :], AF.Silu, [psr[2 * k], rcg], [rg["GS%d" % k]], bias=cgu[:, 2 * f:2 * f + 1])
                    STT(ACTB[:, f, cs], pu[:, :], cgu[:, 2 * f + 1:2 * f + 2], GS[k], ALU.add, ALU.mult,
                        [psr[2 * k + 1], rcg, rg["GS%d" % k]], [ar_(f, hf)])
            for d in range(8):
                s = ws.use(("dd", l, w, d))
                slot = RING[s]
                for hf in range(2):
                    cs = slice(hf * 512, (hf + 1) * 512)
                    k = 4 + y_cnt[0] % 2
                    y_cnt[0] += 1
                    for f in range(NF):
                        MM(PS[k][:, :], slot[:, f * 128:(f + 1) * 128], ACTB[:, f, cs], f == 0, f == NF - 1,
                           [ring_r[s], ar_(f, hf)], [psr[k]])
                    STT(XF[:, d, cs], PS[k][:, :], HG[:, l, 8 * j + d:8 * j + d + 1], XF[:, d, cs], ALU.mult, ALU.add,
                        [psr[k], rg["HG"], xr(d, hf)], [xr(d, hf)])

        def emit_out_proj(nm, src, src_r, gate_ap):
            for d in range(8):
                s = ws.use(("lh", nm, d))
                slot = RING[s]
                for hf in range(2):
                    cs = slice(hf * 512, (hf + 1) * 512)
                    k = 4 + y_cnt[0] % 2
                    y_cnt[0] += 1
                    for kc in range(8):
                        MM(PS[k][:, :], slot[:, kc * 128:(kc + 1) * 128], src[:, kc, cs], kc == 0, kc == 7,
                           [ring_r[s], src_r(kc, hf)], [psr[k]])
                    STT(XF[:, d, cs], PS[k][:, :], gate_ap[:, d:d + 1], XF[:, d, cs], ALU.mult, ALU.add,
                        [psr[k], rg["HG"], xr(d, hf)], [xr(d, hf)])

        EPSB = LAMV[:, 6:7]
        MEMSET("dve", EPSB, EPS, [rg["LAMV"]])

        xl_cnt = [0]
        VTB = ACTa[:, 0:8192].rearrange("p (b n) -> p b n", b=8)
        KNALL = ACTa[:, 8192:16384].rearrange("p (h n) -> p h n", h=8)
        QNALL = ACTa[:, 16384:20480].rearrange("p (h n) -> p h n", h=8)
        for T in range(NT1):
            first = T == 0
            for blk in range(8):
                while xl_cnt[0] < min(NT1 * 8, T * 8 + blk + 4):
                    g = xl_cnt[0]
                    DMA("sp", XIN[g % 4], xall[g * 128:(g + 1) * 128, :], rg["XIN%d" % (g % 4)], [], [rg["XIN%d" % (g % 4)]])
                    xl_cnt[0] += 1
                xi = XIN[blk % 4]
                xir = rg["XIN%d" % (blk % 4)]
                for c4 in range(2):
                    for cc in range(4):
                        c = c4 * 4 + cc
                        TR(PS[7][:, cc * 128:(cc + 1) * 128], xi[:, c * 128:(c + 1) * 128], IDF[:, :], [xir, rg["IDF"]], [psr[7]])
                    hf = blk // 4
                    CP("act", XF[:, c4 * 4:c4 * 4 + 4, blk * 128:(blk + 1) * 128], PS[7][:, :].rearrange("p (c t) -> p c t", c=4),
                       [psr[7]], [xr(c, hf) for c in range(c4 * 4, c4 * 4 + 4)])
            emit_ffn(0, 0, first)
            DMA("sp", X1[T], XF[:, :, 0:512], rg["X1st"], [xr(c, 0) for c in range(8)], [rg[("X1w", T)]])
            emit_norm(GM[:, 0, 8:16], HB, hr)
            units = []
            for qk in (1, 0):
                for h in range(H):
                    for hf in ((0, 1) if qk == 1 else (0,)):
                        units.append((qk, h, hf))
            ustate = {}

            def kq_a(u):
                qk, h, hf = units[u]
                col = qk * 8 + h
                if hf == 0:
                    s = ws.use(("lh", "qkv", col))
                    ustate["slot"] = s
                    if first:
                        for kc in range(8):
                            MM(PS[7][:, col:col + 1], RING[s][:, kc * 128:(kc + 1) * 128], SHB[:, 0, 8 + kc:9 + kc], kc == 0, kc == 7,
                               [ring_r[s], rg["SHB"]], [psr[7]])
                        CP("dve", CQK[:, col:col + 1], PS[7][:, col:col + 1], [psr[7]], [rg["CQK"]])
                s = ustate["slot"]
                slot = RING[s]
                cs = slice(hf * 512, (hf + 1) * 512)
                k = u % 2
                for kc in range(8):
                    MM(PS[2 * k][:, :], slot[:, kc * 128:(kc + 1) * 128], HB[:, kc, cs], kc == 0, kc == 7,
                       [ring_r[s], hr(kc, hf)], [psr[2 * k]])
                ACT(KF[k], PS[2 * k][:, :], AF.Identity, [psr[2 * k], rg["CQK"]], [rg["KF%d" % k]], bias=CQK[:, col:col + 1])
                ACT(GS[k], KF[k], AF.Square, [rg["KF%d" % k]], [rg["GS%d" % k]])

            def kq_b(u):
                qk, h, hf = units[u]
                k = u % 2
                MM(PS[2 * k + 1][:, :], BD64[:, :], GS[k], True, True, [rg["BD64"], rg["GS%d" % k]], [psr[2 * k + 1]])
                RSTD_ACT(RSTD[k], RS, rg["RS"], PS[2 * k + 1][:, :], 1.0 / 64, [psr[2 * k + 1]], [rg["RSTD%d" % k]])
                if qk == 1:
                    STT(KNALL[:, h, hf * 512:(hf + 1) * 512], KF[k], AVEC[:, qk:qk + 1], RSTD[k], ALU.mult, ALU.mult,
                        [rg["KF%d" % k], rg["AVEC"], rg["RSTD%d" % k]], [rg["KNALL%d" % hf]])
                else:
                    STT(QNALL[:, h, :], KF[k], AVEC[:, qk:qk + 1], RSTD[k], ALU.mult, ALU.mult,
                        [rg["KF%d" % k], rg["AVEC"], rg["RSTD%d" % k]], [rg["QNALL"]])

            kq_a(0)
            for u in range(len(units)):
                if u + 1 < len(units):
                    kq_a(u + 1)
                kq_b(u)
            for hf in range(2):
                pos0 = (4 * T) if hf == 0 else (NOWN + 4 * T)
                DMA("sp", KT[:, :, pos0 * 128:pos0 * 128 + 512].rearrange("h p n -> p h n"), KNALL[:, :, hf * 512:(hf + 1) * 512],
                    rg["KNALL%d" % hf], [rg["KNALL%d" % hf]] + [ar_(f, h2) for f in range(8, 16) for h2 in range(2)], [rg[("KTw", T, hf)]])
            DMA("sp", QT[:, :, T * 512:(T + 1) * 512].rearrange("h p n -> p h n"), QNALL[:, :, :],
                rg["QNALL"], [rg["QNALL"]] + [ar_(f, h2) for f in range(16, 20) for h2 in range(2)], [rg[("QTw", T)]])
            for ct in range(4):
                s = ws.use(("v", ct))
                slot3 = RING[s][:, 0:2048].rearrange("p (t k j) -> p t k j", t=2, k=8)
                if first:
                    for kc in range(8):
                        MM(PS[7][0:1, 256:512], SHB[:, 0, 8 + kc:9 + kc], slot3[:, :, kc, :], kc == 0, kc == 7,
                           [ring_r[s], rg["SHB"]], [psr[7]])
                    CP("dve", CVROW[0:1, ct * 256:(ct + 1) * 256], PS[7][0:1, 256:512], [psr[7]], [rg["BSROW"]])
                    MM(PS[7][:, 0:256], ONESF[0:1, :], CVROW[0:1, ct * 256:(ct + 1) * 256], True, True,
                       [rg["ONESF"], rg["BSROW"]], [psr[7]])
                    CP("dve", CVB[:, ct * 256:(ct + 1) * 256], PS[7][:, 0:256], [psr[7]], [rg["CVB"]])
                for blk in range(8):
                    hf = blk // 4
                    k = gu_cnt[0] % 4
                    gu_cnt[0] += 1
                    for kc in range(8):
                        MM(PS[k][:, 0:256], HB[:, kc, blk * 128:(blk + 1) * 128], slot3[:, :, kc, :], kc == 0, kc == 7,
                           [ring_r[s], hr(kc, hf)], [psr[k]])
                    TT(VTB[:, blk, ct * 256:(ct + 1) * 256], PS[k][:, 0:256], CVB[:, ct * 256:(ct + 1) * 256], ALU.add,
                       [psr[k], rg["CVB"]], [rg["VTB%d" % blk]])
            for blk in range(8):
                pos = (4 * T + blk) if blk < 4 else (NOWN + 4 * T + blk - 4)
                DMA("sp", VV[:, :, pos, :].rearrange("h p e -> p h e"), VTB[:, blk, :].rearrange("p (h e) -> p h e", h=8),
                    rg["VTB%d" % blk], [rg["VTB%d" % blk]] + [ar_(blk, h2) for h2 in range(2)], [rg[("VVw", T, blk)]])

        P.barrier()
        AR.reset()
        KTB = [AR.bf16(NB * 128) for _ in range(2)]
        VB = [AR.bf16(NB * 128).rearrange("p (n e) -> p n e", e=128) for _ in range(2)]
        QAB = [AR.bf16(NOWN * 128) for _ in range(2)]
        PT = [AR.bf16(512) for _ in range(6)]
        PSB = [AR.bf16(512) for _ in range(2)]
        BIAS = AR.f32(24 * 128).rearrange("p (n q) -> p n q", n=24)
        os_off = AR.off
        OS = [[AR.f32(512) for _ in range(4)] for _ in range(2)]
        HK = ARENA[:, os_off:os_off + 24 * 128]
        RSo = AR.f32(512)
        RSo2 = AR.f32(512)
        SQo = AR.bf16(512)
        AOB = [AR.bf16(512) for _ in range(2)]
        TAB33 = AR.f32(8)
        EH = AR.f32(768)
        GSB = AR.f32(768)

        DMA("sp", TAB33[0:32, :], relb, rg["TAB33"], [], [rg["TAB33"]])
        MEMSET("dve", TAB33[32:33, :], -1e30, [rg["TAB33"]])
        DMA("sp", EH[0:33, :], ehot, rg["EH"], [], [rg["EH"]])
        for hb in range(2):
            MM(PS[7][0:8, 0:384], TAB33[0:33, :], EH[0:33, hb * 384:(hb + 1) * 384], True, True, [rg["TAB33"], rg["EH"]], [psr[7]])
            CP("dve", GSB[0:8, hb * 384:(hb + 1) * 384], PS[7][0:8, 0:384], [psr[7]], [rg["GSB"]])
        DMA("sp", GD, GSB[0:8, :], rg["GSB"], [rg["GSB"]], [rg["GD"]])
        for sl in range(3):
            for h in range(H):
                n = sl * 8 + h
                DMA("sp", HK[:, n * 128:(n + 1) * 128], bass.AP(GD.tensor, h * 768 + sl * 256, [[1, 128], [1, 128]]),
                    rg["HK"], [rg["GD"]], [rg["HK"]])
        NB31 = AR.f32(8)
        TS(NB31, B31[:, :], -1.0, None, ALU.mult, None, [rg["B31"]], [rg["NB31"]])
        for n4 in range(6):
            MM(PS[7][:, :], JF[:, :], HK[:, n4 * 512:(n4 + 1) * 512], True, True, [rg["JF"], rg["HK"]], [psr[7]])
            for j4 in range(4):
                n = n4 * 4 + j4
                ACT(BIAS[:, n, :], PS[7][:, j4 * 128:(j4 + 1) * 128], AF.Exp, [psr[7], rg["NB31"]], [rg["BIAS"]],
                    bias=NB31[:, n % 8:n % 8 + 1])
        MEMSET("dve", QAB[0][64:128, :], 0.0, [rg["QA%d" % G] for G in range(NG)])
        MEMSET("dve", QAB[1][0:64, :], 0.0, [rg["QB%d" % G] for G in range(NG)])

        def ada_load(l, jg):
            bb = jg % 2
            sr = rg["PREPF%d" % bb]
            DMA("sp", PF[:, bb, :].rearrange("p (k j) -> p k j", k=8),
                ada_w[l, :, jg * 256:(jg + 1) * 256].rearrange("(k p) j -> p k j", p=128), sr, [], [sr])

        def ada_compute(l, jg):
            bb = jg % 2
            st = PF[:, bb, :]
            sr = rg["PREPF%d" % bb]
            for jj in range(2):
                col = jg * 2 + jj
                for kc in range(8):
                    MM(PS[7][:, col:col + 1], st[:, kc * 256 + jj * 128:kc * 256 + jj * 128 + 128], CACT[:, kc:kc + 1],
                       kc == 0, kc == 7, [sr, rg["CACT"]], [psr[7]])
            TT(MODV[:, l, jg * 2:jg * 2 + 2], PS[7][:, jg * 2:jg * 2 + 2], ADAB[:, l, jg * 2:jg * 2 + 2], ALU.add,
               [psr[7], rg["ADAB"]], [rg["MODV"]])

        def mod_derive(l):
            for j in range(3):
                STT(GM[:, l, 8 * j:8 * j + 8], MODV[:, l, (3 * j + 1) * 8:(3 * j + 2) * 8], 1.0, LNV[:, l, 8 * j:8 * j + 8],
                    ALU.add, ALU.mult, [rg["MODV"], rg["LNV"]], [rg["GM"]])
                CP("dve", SHB[:, l, 8 * j:8 * j + 8], MODV[:, l, 3 * j * 8:3 * j * 8 + 8], [rg["MODV"]], [rg["SHB"]])
                TS(HG[:, l, 8 * j:8 * j + 8], MODV[:, l, (3 * j + 2) * 8:(3 * j + 3) * 8], 1.0 if j == 1 else 0.5, None,
                   ALU.mult, None, [rg["MODV"]], [rg["HG"]])

        s_cnt = [0]
        pt_cnt = [0]
        hg_cnt = [0]
        SRING = (0, 1, 2, 7)
        psb_cnt = [0]
        NHG = H * NG
        deferred = []

        def load_kv(hh):
            bb = hh % 2
            DMA("sp", KTB[bb], KT[hh], rg["KTB%d" % bb], [rg["KT"]], [rg["KTB%d" % bb]])
            DMA("sp", VB[bb], VV[hh], rg["VB%d" % bb], [rg["VV"]], [rg["VB%d" % bb]])

        load_kv(0)
        for h in range(H):
            b = h % 2
            rk, rv, rqa, rqb = rg["KTB%d" % b], rg["VB%d" % b], rg["QA"], rg["QB"]
            if h + 1 < H:
                load_kv(h + 1)
            for G in range(NG):
                DMA("sp", QAB[0][0:64, G * 512:(G + 1) * 512], QT[h, 0:64, G * 512:(G + 1) * 512], rg["QA%d" % G], [rg["QT"]], [rg["QA%d" % G]])
                DMA("sp", QAB[1][64:128, G * 512:(G + 1) * 512], QT[h, 64:128, G * 512:(G + 1) * 512], rg["QB%d" % G], [rg["QT"]], [rg["QB%d" % G]])
            for G in range(NG):
                i0 = 4 * G
                kbs = []
                for jj in range(i0):
                    kbs.append((jj, 0, []))
                for jj in range(i0 - 1):
                    kbs.append((NOWN + jj, 0, []))
                for n in range(4):
                    kbs.append((i0 + n, n, [(2, n)]))
                for n in range(5):
                    jj = i0 - 1 + n
                    if jj < 0:
                        continue
                    segs = []
                    if n - 1 >= 0:
                        segs.append((1, n - 1))
                    if n <= 3:
                        segs.append((0, n))
                    kbs.append((NOWN + jj, max(0, n - 1), segs))
                far_l = [t for t in kbs if not t[2]]
                near_l = sorted([t for t in kbs if t[2]], key=lambda t: t[1])
                if far_l:
                    kbs = [far_l[0]]
                    fi = 1
                    for t in near_l:
                        kbs.append(t)
                        if fi < len(far_l):
                            kbs.append(far_l[fi])
                            fi += 1
                    kbs += far_l[fi:]
                else:
                    kbs = near_l
                items = [(c, kb) for c in range(2) for kb in kbs]
                nk = len(kbs)
                sbank = {}

                def qk(idx):
                    c, (pos, lo, segs) = items[idx]
                    sb_ = SRING[s_cnt[0] % 4]
                    s_cnt[0] += 1
                    sbank[idx] = sb_
                    MM(PS[sb_][:, lo * 128:512], KTB[b][:, pos * 128:(pos + 1) * 128],
                       QAB[c][:, G * 512 + lo * 128:(G + 1) * 512], True, True,
                       [rk, rg["QA%d" % G] if c == 0 else rg["QB%d" % G]], [psr[sb_]])

                den_state = {"held": None, "started": [False, False]}

                den_q = []

                def den_flush(keep):
                    while len(den_q) > keep:
                        c_, rhs_ap, lo_, rd_, last_ = den_q.pop(0)
                        st = den_state["started"]
                        MM(PS[5 + c_][:, lo_ * 128:512], ONESB[:, :], rhs_ap, not st[c_], last_, [rg["ONESB"]] + rd_, [psr[5 + c_]])
                        st[c_] = True

                def den_mm(c, rhs_ap, lo, rd, last):
                    den_q.append((c, rhs_ap, lo, rd, last))

                def pv(idx):
                    c, (pos, lo, segs) = items[idx]
                    sb_ = sbank[idx]
                    p = pt_cnt[0] % 6
                    pt_cnt[0] += 1
                    rp = rg["PT%d" % p]
                    ACT(PT[p][:, lo * 128:512], PS[sb_][:, lo * 128:512], AF.Exp, [psr[sb_]], [rp], scale=0.125)
                    for sl, m in segs:
                        TT(PT[p][:, m * 128:(m + 1) * 128], PT[p][:, m * 128:(m + 1) * 128], BIAS[:, sl * 8 + h, :], ALU.mult,
                           [rp, rg["BIAS"]], [rp])
                    ki = idx % nk
                    last = ki == nk - 1
                    MM(PS[3 + c][:, lo * 128:512], VB[b][:, pos, :], PT[p][:, lo * 128:512], ki == 0, last, [rv, rp], [psr[3 + c]])
                    held = den_state["held"]
                    if held is not None:
                        hp, hrp = held
                        q2 = psb_cnt[0] % 2
                        psb_cnt[0] += 1
                        rs2 = rg["PSUMB%d" % q2]
                        den_state["held"] = None
                        if lo == 0:
                            TT(PSB[q2], PT[hp], PT[p], ALU.add, [hrp, rp], [rs2])
                            den_mm(c, PSB[q2], 0, [rs2], last)
                            return
                        den_mm(c, PT[hp], 0, [hrp], False)
                    nxt_same = (not last) and lo == 0
                    if nxt_same:
                        den_state["held"] = (p, rp)
                    else:
                        den_mm(c, PT[p][:, lo * 128:512], lo, [rp], last)

                LOOK = 3
                for idx in range(min(LOOK, len(items))):
                    qk(idx)
                for idx in range(len(items)):
                    pv(idx)
                    den_flush(1)
                    if idx + LOOK < len(items):
                        qk(idx + LOOK)
                    while deferred and deferred[0][0] <= idx:
                        deferred.pop(0)[1]()
                den_flush(0)
                while deferred:
                    deferred.pop(0)[1]()
                ob = (h * NG + G) % 2
                o0, o1, d0, d1 = OS[ob]
                CP("dve", d0, PS[5][:, :], [psr[5]], [rg["OSd0%d" % ob]])
                CP("dve", d1, PS[6][:, :], [psr[6]], [rg["OSd1%d" % ob]])
                CP("dve", o0, PS[3][:, :], [psr[3]], [rg["OSo0%d" % ob]])
                CP("dve", o1, PS[4][:, :], [psr[4]], [rg["OSo1%d" % ob]])
                RECIP(d0, d0, [rg["OSd0%d" % ob]], [rg["OSd0%d" % ob]])
                RECIP(d1, d1, [rg["OSd1%d" % ob]], [rg["OSd1%d" % ob]])
                TT(o0, o0, d0, ALU.mult, [rg["OSo0%d" % ob], rg["OSd0%d" % ob]], [rg["OSo0%d" % ob]])
                TT(o1, o1, d1, ALU.mult, [rg["OSo1%d" % ob], rg["OSd1%d" % ob]], [rg["OSo1%d" % ob]])
                STT(o0, o1, NLAM, o0, ALU.mult, ALU.add, [rg["OSo0%d" % ob], rg["OSo1%d" % ob], rg["LAMV"]], [rg["OSo0%d" % ob]])

                def fin_a(ob=ob, o0=o0):
                    ACT(SQo, o0, AF.Square, [rg["OSo0%d" % ob]], [rg["SQo"]])

                def fin_b(ob=ob):
                    fb = SRING[s_cnt[0] % 4]
                    s_cnt[0] += 1
                    MM(PS[fb][:, :], ONESB[:, :], SQo, True, True, [rg["ONESB"], rg["SQo"]], [psr[fb]])
                    RSTD_ACT(RSo, RSo2, rg["RSo2"], PS[fb][:, :], 1.0 / 128, [psr[fb]], [rg["RSo"]])

                def fin_d(ob=ob, o0=o0, h=h, G=G):
                    STT(AOB[ob], o0, GSUB, RSo, ALU.mult, ALU.mult, [rg["OSo0%d" % ob], rg["LAMV"], rg["RSo"]], [rg["AOB%d" % ob]])
                    DMA("sp", AO[h, :, G * 512:(G + 1) * 512], AOB[ob], rg["AOB%d" % ob], [rg["AOB%d" % ob]], [rg[("AOw", h, G)]])

                deferred.extend([(14, fin_a), (19, fin_b), (24, fin_d)])
        while deferred:
            deferred.pop(0)[1]()

        P.barrier()
        UB = ACTa[:, 0:8192].rearrange("p (c t) -> p c t", c=8)
        VBf = ACTa[:, 8192:16384].rearrange("p (c t) -> p c t", c=8)
        VTK = ACTa[:, 16384:20480]
        yev = []
        for T2 in range(NT2):
            first = T2 == 0
            for hf in range(2):
                DMA("sp", XF[:, :, hf * 512:(hf + 1) * 512], X1[2 * T2 + hf], rg["XFld%d" % hf], [rg["X1"]], [xr(c, hf) for c in range(8)])
            for hf in range(2):
                DMA("sp", HB[:, :, hf * 512:(hf + 1) * 512], AO[:, :, T2 * 1024 + hf * 512:T2 * 1024 + (hf + 1) * 512].rearrange("h p n -> p h n"),
                    rg["HBld%d" % hf], [rg["AO"]], [hr(c, hf) for c in range(8)])
            emit_out_proj("wo", HB, hr, HG[:, 0, 8:16])
            emit_ffn(0, 1, first)
            emit_norm(LNV[:, 0, 24:32], XF, xr)
            emit_ffn(1, 0, first)
            emit_norm(GM[:, 1, 8:16], HB, hr)
            for f in range(16):
                s = ws.use(("lh", "win", f))
                slot = RING[s]
                if first:
                    for kc in range(8):
                        MM(PS[7][:, f:f + 1], slot[:, kc * 128:(kc + 1) * 128], SHB[:, 1, 8 + kc:9 + kc], kc == 0, kc == 7,
                           [ring_r[s], rg["SHB"]], [psr[7]])
                    CP("dve", CZ[:, f:f + 1], PS[7][:, f:f + 1], [psr[7]], [rg["CZ"]])
                for hf in range(2):
                    cs = slice(hf * 512, (hf + 1) * 512)
                    k = gu_cnt[0] % 4
                    gu_cnt[0] += 1
                    for kc in range(8):
                        MM(PS[k][:, :], slot[:, kc * 128:(kc + 1) * 128], HB[:, kc, cs], kc == 0, kc == 7, [ring_r[s], hr(kc, hf)], [psr[k]])
                    dst = UB[:, f, cs] if f < 8 else VBf[:, f - 8, cs]
                    ACT(dst, PS[k][:, :], AF.Gelu, [psr[k], rg["CZ"]], [rg["Z%d_%d" % (f, hf)]], bias=CZ[:, f:f + 1])
            for hf in range(2):
                cs = slice(hf * 512, (hf + 1) * 512)
                vrs = [rg["Z%d_%d" % (8 + c, hf)] for c in range(8)]
                for c in range(8):
                    MM(PS[6][:, :], ONESB[:, :], VBf[:, c, cs], c == 0, c == 7, [rg["ONESB"], vrs[c]], [psr[6]])
                for c in range(8):
                    STT(VC[:, c, :], PS[6][:, :], -1.0 / 1024, VBf[:, c, cs], ALU.mult, ALU.add, [psr[6], vrs[c]], [rg["VC"]])
                ACT(SQ[:, :, :], VC[:, :, :], AF.Square, [rg["VC"]], [rg["SQ"]])
                for c in range(8):
                    MM(PS[6][:, :], ONESB[:, :], SQ[:, c, :], c == 0, c == 7, [rg["ONESB"], rg["SQ"]], [psr[6]])
                RSTD_ACT(RSTD[hf], RS, rg["RS"], PS[6][:, :], 1.0 / 1024, [psr[6]], [rg["RSTD%d" % hf]])
                for c in range(8):
                    STT(VBf[:, c, cs], VC[:, c, :], SGV[:, c:c + 1], RSTD[hf], ALU.mult, ALU.mult,
                        [rg["VC"], rg["SGV"], rg["RSTD%d" % hf]], [vrs[c]])
                vtk = VTK.rearrange("p (n g c) -> p n g c", n=4, g=8)
                for n in range(4):
                    for g4 in range(2):
                        pbf = PS[7][:, 0:256].bitcast(BF16)
                        for gg in range(4):
                            g = g4 * 4 + gg
                            TR(pbf[:, gg * 128:(gg + 1) * 128], VBf[:, g, hf * 512 + n * 128:hf * 512 + (n + 1) * 128], IDB[:, :],
                               [vrs[g], rg["IDB"]], [psr[7]])
                        CP("act", vtk[:, n, g4 * 4:g4 * 4 + 4, :], pbf.rearrange("p (g c) -> p g c", g=4), [psr[7]], [rg["VTK"]])
                for g in range(8):
                    k = gu_cnt[0] % 4
                    gu_cnt[0] += 1
                    for n in range(4):
                        MM(PS[k][:, n * 128:(n + 1) * 128], vtk[:, n, g, :], WMT[:, g, :], True, True, [rg["VTK"], rg["WMT"]], [psr[k]])
                    TT(VC[:, g, :].rearrange("p (n t) -> p n t", n=4), PS[k][:, :].rearrange("p (n t) -> p n t", n=4),
                       B2[:, g, :].unsqueeze(1).to_broadcast([128, 4, 128]), ALU.add, [psr[k], rg["B2"]], [rg["VC"]])
                    TT(VBf[:, g, cs], VC[:, g, :], UB[:, g, cs], ALU.mult, [rg["VC"], rg["Z%d_%d" % (g, hf)]], [vrs[g]])
            emit_out_proj("wout", VBf, lambda c, hf: rg["Z%d_%d" % (8 + c, hf)], HG[:, 1, 8:16])
            emit_ffn(1, 1, first)
            emit_norm(LNV[:, 1, 24:32], XF, xr)
            for blk in range(8):
                yt = YT[blk % 4]
                ytr = rg["XIN%d" % (blk % 4)]
                hf = blk // 4
                for c4 in range(2):
                    k = gu_cnt[0] % 4
                    gu_cnt[0] += 1
                    for cc in range(4):
                        c = c4 * 4 + cc
                        TR(PS[k][:, cc * 128:(cc + 1) * 128], XF[:, c, blk * 128:(blk + 1) * 128], IDF[:, :], [xr(c, hf), rg["IDF"]], [psr[k]])
                    CP("act" if c4 == 0 else "dve", yt[:, c4 * 512:(c4 + 1) * 512], PS[k][:, :], [psr[k]], [ytr])
                row0 = T2 * 1024 + blk * 128
                yev.append(DMA("sp", y[row0:row0 + 128, :], yt, ytr, [ytr], [rg[("Yw", T2, blk)]]))
        P.wait("sp", yev)

    ws = WStream()
    emit_all(Prog(dry=True), ws)
    ws.collect = False
    P = Prog()
    emit_all(P, ws)
    assert ws.ptr == len(ws.keys)
    P.emit(nc)
    es.close()
    return nc


def _t5_bucket(rel):
    n = np.maximum(rel, 0)
    nf = np.maximum(n, 1).astype(np.float32)
    large = 16 + (np.log(nf / np.float32(16)) / np.float32(math.log(8.0)) * np.float32(16)).astype(np.int32)
    large = np.minimum(large, 31)
    return np.where(n < 16, n, large)


def host_inputs(inp, S, cores):
    NB = S // 128
    NT1 = S // 1024
    f32 = np.float32
    c = lambda a: np.ascontiguousarray(a, dtype=f32)
    common = {
        "ada_w": c(inp["ada_w"]),
        "ada_b2": c(inp["ada_b"].reshape(2, 72, 128).transpose(2, 0, 1)),
        "lnv": c(np.stack([inp["ln_ffn1"], inp["ln_mix"], inp["ln_ffn2"], inp["ln_out"]], 1).reshape(2, 4, 8, 128)
                 .transpose(3, 0, 1, 2).reshape(128, 2, 32)),
        "wg1": c(inp["ffn1_wg"]), "wu1": c(inp["ffn1_wu"]), "wd1": c(inp["ffn1_wd"]),
        "wg2": c(inp["ffn2_wg"]), "wu2": c(inp["ffn2_wu"]), "wd2": c(inp["ffn2_wd"]),
        "wqkv": c(inp["attn_w_qkv"][0]), "wo": c(inp["attn_w_o"][0]),
        "w_in": c(inp["sgu_w_in"][0]), "w_out": c(inp["sgu_w_out"][0]),
        "avec": c(np.stack([np.tile(inp["attn_q_norm"][0], 2), np.tile(inp["attn_k_norm"][0], 2), inp["attn_subln"][0],
                            np.zeros(128, f32)], 1)),
        "lrow": c(np.concatenate([inp["attn_lq1"][0], inp["attn_lk1"][0], inp["attn_lq2"][0], inp["attn_lk2"][0]])[None, :]),
        "sgv": c(np.concatenate([inp["sgu_ln_g"][0].reshape(8, 128).T, inp["sgu_ln_b"][0].reshape(8, 128).T], 1)),
        "w_s": c(inp["sgu_w_s"][0]),
        "bsrow": c(inp["sgu_b_s"][0].reshape(1, 1024)),
        "relb": c(inp["rel_bias"]),
        "ident": np.eye(128, dtype=f32),
        "tril": np.tril(np.ones((128, 128), f32)),
        "jflip": np.ascontiguousarray(np.eye(128, dtype=f32)[::-1]),
        "bd64": np.kron(np.eye(2, dtype=f32), np.ones((64, 64), f32)),
    }
    ehots = []
    for r in range(2):
        dists = (128, -128, 0) if r == 0 else (384, 128, 0)
        e = np.zeros((33, 768), f32)
        for sl, dd in enumerate(dists):
            j = np.arange(255)
            rel = j - 127 + dd
            bk = _t5_bucket(rel)
            for jj in range(255):
                if rel[jj] >= 0:
                    e[bk[jj], sl * 256 + jj] = 1.0
                else:
                    e[32, sl * 256 + jj] = 1.0
        ehots.append(e)
    maps = []
    perms = []
    for core in cores:
        b, r = core // 2, core % 2
        order = []
        for T in range(NT1):
            order += [2 * j + r for j in range(4 * T, 4 * T + 4)]
            order += [2 * j + 1 - r for j in range(4 * T, 4 * T + 4)]
        xb = inp["x"][b].reshape(NB, 128, D)
        m = dict(common)
        m["xall"] = c(xb[order].reshape(S, D))
        m["ccol"] = c(inp["c"][b].reshape(8, 128).T)
        m["ehot"] = ehots[r]
        maps.append(m)
        perms.append((b, r))
    return maps, perms


_NC_CACHE = {}


def kernel(**inputs):
    S = inputs["x"].shape[1]
    B = inputs["x"].shape[0]
    cores = list(range(2 * B))
    if S not in _NC_CACHE:
        _NC_CACHE[S] = build(S)
    nc = _NC_CACHE[S]
    maps, perms = host_inputs(inputs, S, cores)
    res = run_bass_kernel_spmd(nc, maps, core_ids=cores)
    NB = S // 128
    out = np.empty((B, NB, 128, D), np.float32)
    for (b, r), rr in zip(perms, res.results):
        out[b, r::2] = np.asarray(rr["y"]).reshape(NB // 2, 128, D)
    return out.reshape(B, S, D)
```

```python
import math
from contextlib import ExitStack

import numpy as np
import concourse.bass as bass
import concourse.mybir as mybir
from concourse.bass_utils import run_bass_kernel_spmd

F32 = mybir.dt.float32
BF16 = mybir.dt.bfloat16
AF = mybir.ActivationFunctionType
ALU = mybir.AluOpType
AX = mybir.AxisListType

D = 1024
NKC = 8
FF = 2816
NF = 22
H = 8
EPS = 1e-6
COMPUTE = ("pe", "act", "dve", "pool")
SLOT_B = 5632
DEPTH = 5


class Reg:
    __slots__ = ("name", "w", "r", "cnt")

    def __init__(self, name):
        self.name = name
        self.w = None
        self.r = []
        self.cnt = 0


class RegDict(dict):
    def __missing__(self, k):
        v = Reg(str(k))
        self[k] = v
        return v


class Prog:
    def __init__(self, dry=False):
        self.dry = dry
        self.streams = {e: [] for e in ("pe", "act", "dve", "pool", "sp")}
        self.count = {e: 0 for e in COMPUTE}
        self.seen = {e: {} for e in self.streams}
        self.anchors = []

    def _need(self, eng, waits, ev, raw):
        key, val = ev
        if key == eng and not raw and eng == "pe":
            return
        if self.seen[eng].get(key, 0) >= val:
            return
        if waits.get(key, 0) < val:
            waits[key] = val

    def _deps(self, eng, reads, writes, anchor=None):
        waits = {}
        for R in reads:
            if R.w is not None:
                self._need(eng, waits, R.w, True)
        for R in writes:
            if R.w is not None:
                if not (anchor is not None and R.w[0] is anchor and not R.r):
                    self._need(eng, waits, R.w, False)
            for ev in R.r:
                self._need(eng, waits, ev, False)
        for k, v in waits.items():
            self.seen[eng][k] = v
        return waits

    def _mark(self, ev, reads, writes):
        for R in writes:
            R.w = ev
            R.r = []
        for R in reads:
            if R not in writes:
                R.r.append(ev)

    def op(self, eng, fn, reads=(), writes=()):
        if self.dry:
            return None
        waits = self._deps(eng, reads, writes)
        self.count[eng] += 1
        ev = (eng, self.count[eng])
        self.streams[eng].append(("op", fn, waits, None))
        self._mark(ev, reads, writes)
        return ev

    def dma(self, queue, out, in_, anchor, reads=(), writes=()):
        if self.dry:
            return None
        waits = self._deps(queue, reads, writes, anchor)
        if anchor.cnt == 0:
            self.anchors.append(anchor)
        anchor.cnt += 16
        ev = (anchor, anchor.cnt)
        self.streams[queue].append(("dma", (out, in_), waits, anchor))
        self._mark(ev, reads, writes)
        return ev

    def wait(self, eng, events):
        if self.dry:
            return
        waits = {}
        for ev in events:
            if ev is not None:
                self._need(eng, waits, ev, True)
        for k, v in waits.items():
            self.seen[eng][k] = v
        if waits:
            self.streams[eng].append(("wait", None, waits, None))

    def barrier(self):
        if self.dry:
            return
        evs = [(e, self.count[e]) for e in COMPUTE if self.count[e] > 0 and e != "pool"]
        evs += [(a, a.cnt) for a in self.anchors if not a.name.startswith("PREP")]
        for e in self.streams:
            if e != "pool":
                self.wait(e, evs)

    def emit(self, nc):
        with ExitStack() as es:
            sems = {}
            for e in COMPUTE:
                sems[e] = es.enter_context(nc.semaphore("s_" + e))
            for i, a in enumerate(self.anchors):
                sems[a] = es.enter_context(nc.semaphore("d%d" % i))
            block = es.enter_context(nc.Block())

            def run(engname):
                def body(e):
                    for kind, fn, waits, anchor in self.streams[engname]:
                        for k, v in waits.items():
                            e.wait_ge(sems[k], v)
                        if kind == "op":
                            fn(e).then_inc(sems[engname], 1)
                        elif kind == "dma":
                            e.dma_start(out=fn[0], in_=fn[1]).then_inc(sems[anchor], 16)
                return body

            block.tensor(run("pe"))
            block.scalar(run("act"))
            block.vector(run("dve"))
            block.gpsimd(run("pool"))
            block.sync(run("sp"))


class WStream:
    def __init__(self):
        self.keys = []
        self.collect = True
        self.ptr = 0
        self.loaded = 0
        self.loader = None

    def use(self, key):
        if self.collect:
            self.keys.append(key)
            return 0
        i = self.ptr
        assert self.keys[i] == key, (self.keys[i], key)
        self.ptr += 1
        while self.loaded < min(len(self.keys), i + DEPTH):
            self.loader(self.keys[self.loaded], self.loaded % DEPTH)
            self.loaded += 1
        return i % DEPTH


def build(S, debug=False):
    NB = S // 128
    NOWN = NB // 2
    NT1 = S // 1024
    NT2 = S // 2048
    NG = NOWN // 4
    nc = bass.Bass("TRN2", target_bir_lowering=False)

    def din(name, shape, dt=F32):
        return nc.dram_tensor(name, shape, dt, kind="ExternalInput").ap()

    def dscr(name, shape, dt):
        return nc.dram_tensor(name, shape, dt, kind="ExternalOutput" if debug else "Internal").ap()

    xall = din("xall", [S, D])
    ccol = din("ccol", [128, 8])
    ada_w = din("ada_w", [2, D, 9 * D])
    ada_b2 = din("ada_b2", [128, 2, 72])
    lnv = din("lnv", [128, 2, 32])
    wg = [din("wg1", [2, D, FF]), din("wg2", [2, D, FF])]
    wu = [din("wu1", [2, D, FF]), din("wu2", [2, D, FF])]
    wd = [din("wd1", [2, FF, D]), din("wd2", [2, FF, D])]
    wqkv = din("wqkv", [D, 3 * D])
    wo = din("wo", [D, D])
    w_in = din("w_in", [D, 2 * D])
    w_out = din("w_out", [D, D])
    avec = din("avec", [128, 4])
    lrow = din("lrow", [1, 256])
    sgv = din("sgv", [128, 16])
    w_s = din("w_s", [8, 128, 128])
    bsrow = din("bsrow", [1, 1024])
    relb = din("relb", [32, 8])
    ident = din("ident", [128, 128])
    tril = din("tril", [128, 128])
    ehot = din("ehot", [33, 768])
    jflip = din("jflip", [128, 128])
    bd64 = din("bd64", [128, 128])
    y = nc.dram_tensor("y", [NOWN * 128, D], F32, kind="ExternalOutput").ap()

    GU = [[dscr("GU%d%d" % (l, w), [NF, 128, 2, 1024], BF16) for w in range(2)] for l in range(2)]
    DD = [[dscr("DD%d%d" % (l, w), [8, 128, NF * 128], BF16) for w in range(2)] for l in range(2)]
    LH = {"qkv": dscr("LHqkv", [24, 128, 1024], BF16), "wo": dscr("LHwo", [8, 128, 1024], BF16),
          "win": dscr("LHwin", [16, 128, 1024], BF16), "wout": dscr("LHwout", [8, 128, 1024], BF16)}
    KT = dscr("KT", [H, 128, NB * 128], BF16)
    VV = dscr("VV", [H, 128, NB, 128], BF16)
    QT = dscr("QT", [H, 128, NOWN * 128], BF16)
    X1 = dscr("X1", [NT1, 128, 8, 512], F32)
    AO = dscr("AO", [H, 128, NOWN * 128], BF16)
    GD = dscr("GD", [8, 768], F32)
    DBG = dscr("DBG", [128, 8, 1024], F32) if debug else None

    es = ExitStack()

    def sbt(name, shape, dt):
        return es.enter_context(nc.sbuf_tensor(name, shape, dt))

    IDF = sbt("IDF", [128, 128], F32)
    IDB = sbt("IDB", [128, 128], BF16)
    ONESB = sbt("ONESB", [128, 128], BF16)
    BD64 = sbt("BD64", [128, 128], BF16)
    ONESF = sbt("ONESF", [128, 128], F32)
    TRIL = sbt("TRIL", [128, 128], F32)
    JF = sbt("JF", [128, 128], F32)
    CCOL = sbt("CCOL", [128, 8], F32)
    CACT = sbt("CACT", [128, 8], F32)
    MODV = sbt("MODV", [128, 2, 72], F32)
    ADAB = sbt("ADAB", [128, 2, 72], F32)
    LNV = sbt("LNV", [128, 2, 32], F32)
    GM = sbt("GM", [128, 2, 24], F32)
    SHB = sbt("SHB", [128, 2, 24], BF16)
    HG = sbt("HG", [128, 2, 24], F32)
    CGU = sbt("CGU", [128, 4, 44], F32)
    CQK = sbt("CQK", [128, 16], F32)
    CVB = sbt("CVB", [128, 1024], F32)
    CZ = sbt("CZ", [128, 16], F32)
    AVEC = sbt("AVEC", [128, 4], F32)
    LAMV = sbt("LAMV", [128, 8], F32)
    LROW = sbt("LROW", [128, 256], F32)
    LTMP = sbt("LTMP", [128, 128], F32)
    SGV = sbt("SGV", [128, 16], F32)
    WMT = sbt("WMT", [128, 8, 128], BF16)
    B2 = sbt("B2", [128, 8, 128], F32)
    B31 = sbt("B31", [128, 8], F32)
    BSROW = sbt("BSROW", [128, 1024], F32)
    PF = sbt("PF", [128, 2, 2048], F32)
    PBF = sbt("PBF", [128, 2, 2048], BF16)
    CVROW = BSROW
    RINGT = sbt("RINGT", [128, DEPTH, SLOT_B // 2], BF16)
    ARW = 33500
    ARENA = sbt("ARENA", [128, ARW], F32)
    PS = [es.enter_context(nc.psum_tensor("PS%d" % i, [128, 512], F32)) for i in range(8)]

    class Arena:
        def __init__(self):
            self.off = 0

        def reset(self):
            self.off = 0

        def f32(self, n):
            a = ARENA[:, self.off:self.off + n]
            self.off += n
            assert self.off <= ARW, self.off
            return a

        def bf16(self, n):
            assert n % 2 == 0
            a = ARENA[:, self.off:self.off + n // 2].bitcast(BF16)
            self.off += n // 2
            assert self.off <= ARW, self.off
            return a

    AR = Arena()

    def emit_all(P, ws):
        rg = RegDict()
        psr = [rg["PS%d" % i] for i in range(8)]

        def MM(out, lhsT, rhs, start, stop, rd, wr):
            P.op("pe", lambda e: e.matmul(out, lhsT=lhsT, rhs=rhs, start=start, stop=stop), rd, wr)

        def TR(out, in_, idn, rd, wr):
            P.op("pe", lambda e: e.transpose(out=out, in_=in_, identity=idn), rd, wr)

        def ACT(out, in_, func, rd, wr, bias=None, scale=None):
            kw = {}
            if bias is not None:
                kw["bias"] = bias
            if scale is not None:
                kw["scale"] = scale
            P.op("act", lambda e: e.activation(out=out, in_=in_, func=func, **kw), rd, wr)

        def STT(out, in0, scalar, in1, op0, op1, rd, wr):
            P.op("dve", lambda e: e.scalar_tensor_tensor(out=out, in0=in0, scalar=scalar, in1=in1, op0=op0, op1=op1), rd, wr)

        def TT(out, in0, in1, op, rd, wr, eng="dve"):
            P.op(eng, lambda e: e.tensor_tensor(out=out, in0=in0, in1=in1, op=op), rd, wr)

        def TS(out, in0, s1, s2, op0, op1, rd, wr, eng="dve"):
            if s2 is None:
                P.op(eng, lambda e: e.tensor_scalar(out=out, in0=in0, scalar1=s1, scalar2=None, op0=op0), rd, wr)
            else:
                P.op(eng, lambda e: e.tensor_scalar(out=out, in0=in0, scalar1=s1, scalar2=s2, op0=op0, op1=op1), rd, wr)

        def CP(eng, out, in_, rd, wr):
            if eng == "act":
                P.op("act", lambda e: e.activation(out=out, in_=in_, func=AF.Copy), rd, wr)
            else:
                P.op(eng, lambda e: e.tensor_copy(out=out, in_=in_), rd, wr)

        def RSTD_ACT(out, tmp, tmp_r, in_, scale, rd, wr):
            ACT(tmp, in_, AF.Ln, rd, [tmp_r], bias=EPSB[:, 0:1], scale=scale)
            ACT(out, tmp, AF.Exp, [tmp_r], wr, scale=-0.5)

        def RECIP(out, in_, rd, wr):
            P.op("dve", lambda e: e.reciprocal(out=out, in_=in_), rd, wr)

        def MEMSET(eng, ap, val, wr):
            P.op(eng, lambda e: e.memset(ap, val), (), wr)

        def DMA(q, out, in_, anchor, rd, wr):
            return P.dma(q, out, in_, anchor, rd, wr)

        for name, dst, src in (("IDF", IDF, ident), ("TRIL", TRIL, tril), ("JF", JF, jflip), ("CCOL", CCOL, ccol),
                               ("ADAB", ADAB, ada_b2), ("LNV", LNV, lnv), ("AVEC", AVEC, avec), ("SGV", SGV, sgv)):
            DMA("sp", dst[:], src, rg[name], [], [rg[name]])
        DMA("sp", LROW[:, :], lrow.partition_broadcast(128), rg["LROW"], [], [rg["LROW"]])
        DMA("sp", B31[:, :], relb[31:32, :].partition_broadcast(128), rg["B31"], [], [rg["B31"]])
        DMA("sp", BSROW[0:1, :], bsrow, rg["BSROW"], [], [rg["BSROW"]])
        AR.reset()
        bdf = AR.f32(128)
        DMA("sp", bdf, bd64, rg["bdf"], [], [rg["bdf"]])
        CP("dve", BD64[:, :], bdf, [rg["bdf"]], [rg["BD64"]])
        CP("dve", IDB[:, :], IDF[:, :], [rg["IDF"]], [rg["IDB"]])
        MEMSET("dve", ONESB[:, :], 1.0, [rg["ONESB"]])
        MEMSET("dve", ONESF[:, :], 1.0, [rg["ONESF"]])
        ACT(CACT[:, :], CCOL[:, :], AF.Silu, [rg["CCOL"]], [rg["CACT"]])
        for t in range(2):
            TT(LTMP[:, 0:64], LROW[:, 128 * t:128 * t + 64], LROW[:, 128 * t + 64:128 * t + 128], ALU.mult,
               [rg["LROW"]], [rg["LTMP"]])
            P.op("dve", lambda e, t=t: e.reduce_sum(out=LAMV[:, t:t + 1], in_=LTMP[:, 0:64], axis=AX.X),
                 [rg["LTMP"]], [rg["LAMV"]])
        ACT(LAMV[:, 2:4], LAMV[:, 0:2], AF.Exp, [rg["LAMV"]], [rg["LAMV"]])
        TT(LAMV[:, 4:5], LAMV[:, 3:4], LAMV[:, 2:3], ALU.subtract, [rg["LAMV"]], [rg["LAMV"]])
        TS(LAMV[:, 4:5], LAMV[:, 4:5], -0.2, None, ALU.add, None, [rg["LAMV"]], [rg["LAMV"]])
        TS(LAMV[:, 5:6], AVEC[:, 2:3], 0.8, None, ALU.mult, None, [rg["AVEC"], rg["LAMV"]], [rg["LAMV"]])
        NLAM = LAMV[:, 4:5]
        GSUB = LAMV[:, 5:6]

        ada_st = [AR.f32(2048) for _ in range(2)]
        for l in range(2):
            for jg in range(36):
                st = ada_st[jg % 2]
                sr = rg["ada_st%d" % (jg % 2)]
                DMA("sp", st.rearrange("p (k j) -> p k j", k=8),
                    ada_w[l, :, jg * 256:(jg + 1) * 256].rearrange("(k p) j -> p k j", p=128), sr, [], [sr])
                for jj in range(2):
                    col = jg * 2 + jj
                    for kc in range(8):
                        MM(PS[7][:, col:col + 1], st[:, kc * 256 + jj * 128:kc * 256 + jj * 128 + 128], CACT[:, kc:kc + 1],
                           kc == 0, kc == 7, [sr, rg["CACT"]], [psr[7]])
            TT(MODV[:, l, :], PS[7][:, 0:72], ADAB[:, l, :], ALU.add, [psr[7], rg["ADAB"]], [rg["MODV"]])
            for j in range(3):
                STT(GM[:, l, 8 * j:8 * j + 8], MODV[:, l, (3 * j + 1) * 8:(3 * j + 2) * 8], 1.0, LNV[:, l, 8 * j:8 * j + 8],
                    ALU.add, ALU.mult, [rg["MODV"], rg["LNV"]], [rg["GM"]])
                CP("dve", SHB[:, l, 8 * j:8 * j + 8], MODV[:, l, 3 * j * 8:3 * j * 8 + 8], [rg["MODV"]], [rg["SHB"]])
                TS(HG[:, l, 8 * j:8 * j + 8], MODV[:, l, (3 * j + 2) * 8:(3 * j + 3) * 8], 1.0 if j == 1 else 0.5, None,
                   ALU.mult, None, [rg["MODV"]], [rg["HG"]])

        wsf = AR.f32(1024)
        wsv = wsf.rearrange("p (g s) -> p g s", g=8)
        DMA("sp", wsv, w_s.rearrange("g t s -> t g s"), rg["wsf"], [], [rg["wsf"]])
        TT(wsv, wsv, TRIL[:, :].unsqueeze(1).to_broadcast([128, 8, 128]), ALU.mult, [rg["wsf"], rg["TRIL"]], [rg["wsf"]])
        wmtf = AR.f32(1024)
        for hb in range(2):
            for g4 in range(4):
                g = hb * 4 + g4
                TR(PS[6][:, g4 * 128:(g4 + 1) * 128], wsf[:, g * 128:(g + 1) * 128], IDF[:, :], [rg["wsf"], rg["IDF"]], [psr[6]])
            CP("dve", wmtf[:, hb * 512:(hb + 1) * 512], PS[6][:, :], [psr[6]], [rg["wmtf"]])
        CP("dve", WMT[:, :, :], wmtf.rearrange("p (g t) -> p g t", g=8), [rg["wmtf"]], [rg["WMT"]])
        for hb in range(2):
            MM(PS[6][:, :], ONESF[:, :], wmtf[:, hb * 512:(hb + 1) * 512], True, True, [rg["ONESF"], rg["wmtf"]], [psr[6]])
            MM(PS[5][:, :], ONESF[0:1, :], BSROW[0:1, hb * 512:(hb + 1) * 512], True, True, [rg["ONESF"], rg["BSROW"]], [psr[5]])
            for g4 in range(4):
                g = hb * 4 + g4
                CP("act", LTMP[:, :], PS[5][:, g4 * 128:(g4 + 1) * 128], [psr[5]], [rg["LTMP"]])
                STT(B2[:, g, :], PS[6][:, g4 * 128:(g4 + 1) * 128], SGV[:, 8 + g:9 + g], LTMP[:, :], ALU.mult, ALU.add,
                    [psr[6], rg["SGV"], rg["LTMP"]], [rg["B2"]])

        pcount = [0]

        def prep_cols(src2d, C, dst, rname, only=None):
            for cg in (range(C // 256) if only is None else (only,)):
                b = pcount[0] % 2
                pcount[0] += 1
                rf, rb = rg["PREPF%d" % b], rg["PREPB%d" % b]
                rdst = rg[(rname, cg)]
                DMA("pool", PF[:, b, :].rearrange("p (k j) -> p k j", k=8),
                    src2d[:, cg * 256:(cg + 1) * 256].rearrange("(k p) j -> p k j", p=128), rf, [], [rf])
                for kk in range(8):
                    P.op("pool", lambda e, b=b, kk=kk: e.tensor_copy(
                        out=PBF[:, b, :].rearrange("p (s k j) -> p s k j", s=2, k=8)[:, :, kk, :],
                        in_=PF[:, b, :].rearrange("p (k s j) -> p s k j", k=8, s=2)[:, :, kk, :]), [rf], [rb])
                DMA("pool", dst[cg * 2:cg * 2 + 2].rearrange("s p n -> p s n"),
                    PBF[:, b, :].rearrange("p (s n) -> p s n", s=2), rb, [rb], [rdst])

        def prep_rows(src2d, dst, rname):
            for fg in range(11):
                b = pcount[0] % 2
                pcount[0] += 1
                rf, rb = rg["PREPF%d" % b], rg["PREPB%d" % b]
                rdst = rg[(rname, fg)]
                DMA("pool", PF[:, b, :].rearrange("p (f n) -> p f n", f=2),
                    src2d[fg * 256:(fg + 1) * 256, :].rearrange("(f p) n -> p f n", p=128), rf, [], [rf])
                for dd_ in range(8):
                    P.op("pool", lambda e, b=b, dd_=dd_: e.tensor_copy(
                        out=PBF[:, b, :].rearrange("p (d f j) -> p d f j", d=8, f=2)[:, dd_, :, :],
                        in_=PF[:, b, :].rearrange("p (f d j) -> p d f j", f=2, d=8)[:, dd_, :, :]), [rf], [rb])
                DMA("pool", dst[:, :, fg * 256:(fg + 1) * 256].rearrange("d p n -> p d n"),
                    PBF[:, b, :].rearrange("p (d n) -> p d n", d=8), rb, [rb], [rdst])

        def prep_ffn(l, w):
            for cg in range(FF // 256):
                for t, srcw in enumerate((wg[w], wu[w])):
                    prep_cols(srcw[l], FF, GU[l][w][:, :, t, :], "GU%d%d_%d" % (l, w, t), only=cg)
            prep_rows(wd[w][l], DD[l][w], "DD%d%d" % (l, w))

        prep_ffn(0, 0)
        prep_cols(wqkv, 3 * D, LH["qkv"], "LHqkv")
        prep_cols(wo, D, LH["wo"], "LHwo")
        prep_ffn(0, 1)
        prep_ffn(1, 0)
        prep_cols(w_in, 2 * D, LH["win"], "LHwin")
        prep_cols(w_out, D, LH["wout"], "LHwout")
        prep_ffn(1, 1)

        P.barrier()
        AR.reset()
        XFa = AR.f32(8192)
        XF = XFa.rearrange("p (c t) -> p c t", c=8)
        HBa = AR.bf16(8192)
        HB = HBa.rearrange("p (c t) -> p c t", c=8)
        ACTa = AR.bf16(NF * 1024)
        ACTB = ACTa.rearrange("p (f t) -> p f t", f=NF)
        RING = [RINGT[:, i, :] for i in range(DEPTH)]
        SQ = AR.bf16(4096).rearrange("p (c t) -> p c t", c=8)
        RS = AR.f32(512)
        RSTD = [AR.f32(512) for _ in range(2)]
        vc_off = AR.off
        KF = [AR.f32(512) for _ in range(2)]
        GS = [AR.bf16(512) for _ in range(2)]
        KN = [AR.bf16(512) for _ in range(2)]
        XIN = [AR.f32(1024) for _ in range(2)]
        YT = XIN
        assert AR.off - vc_off == 4096
        XIN = XIN + [AR.f32(1024) for _ in range(2)]
        YT = XIN
        VC = ARENA[:, vc_off:vc_off + 4096].rearrange("p (c t) -> p c t", c=8)
        ring_r = [rg["RING%d" % s] for s in range(DEPTH)]

        def xr(c, hf):
            return rg["XF%d_%d" % (c, hf)]

        def hr(c, hf):
            return rg["HB%d_%d" % (c, hf)]

        def ar_(f, hf):
            return rg["ACT%d_%d" % (f, hf)]

        def loader(key, s):
            kind = key[0]
            if kind == "gu":
                _, l, w, f = key
                DMA("sp", RING[s][:, 0:2048].rearrange("p (t n) -> p t n", t=2), GU[l][w][f], ring_r[s],
                    [rg[("GU%d%d_%d" % (l, w, t), f // 2)] for t in range(2)], [ring_r[s]])
            elif kind == "dd":
                _, l, w, d = key
                DMA("sp", RING[s][:, 0:NF * 128], DD[l][w][d], ring_r[s], [rg[("DD%d%d" % (l, w), fg)] for fg in range(11)], [ring_r[s]])
            elif kind == "lh":
                _, nm, j = key
                DMA("sp", RING[s][:, 0:1024], LH[nm][j], ring_r[s], [rg[("LH" + nm, j // 2)]], [ring_r[s]])
            elif kind == "v":
                _, ct = key
                DMA("sp", RING[s][:, 0:2048].rearrange("p (t n) -> p t n", t=2),
                    LH["qkv"][16 + 2 * ct:18 + 2 * ct].rearrange("s p n -> p s n"), ring_r[s], [rg[("LHqkv", 8 + ct)]], [ring_r[s]])

        ws.loader = loader
        gu_cnt = [0]
        y_cnt = [0]

        def emit_norm(gain_ap, dst, dst_r):
            for hf in range(2):
                cs = slice(hf * 512, (hf + 1) * 512)
                xrs = [xr(c, hf) for c in range(8)]
                ACT(SQ[:, :, :], XF[:, :, cs], AF.Square, xrs, [rg["SQ"]])
                for c in range(8):
                    MM(PS[6][:, :], ONESB[:, :], SQ[:, c, :], c == 0, c == 7, [rg["ONESB"], rg["SQ"]], [psr[6]])
                RSTD_ACT(RSTD[hf], RS, rg["RS"], PS[6][:, :], 1.0 / D, [psr[6]], [rg["RSTD%d" % hf]])
                for c in range(8):
                    STT(dst[:, c, cs], XF[:, c, cs], gain_ap[:, c:c + 1], RSTD[hf], ALU.mult, ALU.mult,
                        [xr(c, hf), rg["RSTD%d" % hf], rg["GM"], rg["LNV"]], [dst_r(c, hf)])

        def emit_ffn(l, w, first):
            j = 0 if w == 0 else 2
            emit_norm(GM[:, l, 8 * j:8 * j + 8], HB, hr)
            cgu = CGU[:, l * 2 + w, :]
            rcg = rg["CGU%d%d" % (l, w)]
            for f in range(NF):
                s = ws.use(("gu", l, w, f))
                slot = RING[s]
                if first:
                    for t in range(2):
                        for kc in range(8):
                            MM(PS[7][:, 2 * f + t:2 * f + t + 1], slot[:, t * 1024 + kc * 128:t * 1024 + kc * 128 + 128],
                               SHB[:, l, 8 * j + kc:8 * j + kc + 1], kc == 0, kc == 7, [ring_r[s], rg["SHB"]], [psr[7]])
                    CP("dve", cgu[:, 2 * f:2 * f + 2], PS[7][:, 2 * f:2 * f + 2], [psr[7]], [rcg])
                for hf in range(2):
                    cs = slice(hf * 512, (hf + 1) * 512)
                    k = gu_cnt[0] % 2
                    gu_cnt[0] += 1
                    pg, pu = PS[2 * k], PS[2 * k + 1]
                    for t, pp in ((0, pg), (1, pu)):
                        for kc in range(8):
                            MM(pp[:, :], slot[:, t * 1024 + kc * 128:t * 1024 + kc * 128 + 128], HB[:, kc, cs], kc == 0, kc == 7,
                               [ring_r[s], hr(kc, hf)], [psr[2 * k + t]])
                    ACT(GS[k], pg[:, :], AF.Silu, [psr[2 * k], rcg], [rg["GS%d" % k]], bias=cgu[:, 2 * f:2 * f + 1])
                    STT(ACTB[:, f, cs], pu[:, :], cgu[:, 2 * f + 1:2 * f + 2], GS[k], ALU.add, ALU.mult,
                        [psr[2 * k + 1], rcg, rg["GS%d" % k]], [ar_(f, hf)])
            for d in range(8):
                s = ws.use(("dd", l, w, d))
                slot = RING[s]
                for hf in range(2):
                    cs = slice(hf * 512, (hf + 1) * 512)
                    k = 4 + y_cnt[0] % 2
                    y_cnt[0] += 1
                    for f in range(NF):
                        MM(PS[k][:, :], slot[:, f * 128:(f + 1) * 128], ACTB[:, f, cs], f == 0, f == NF - 1,
                           [ring_r[s], ar_(f, hf)], [psr[k]])
                    STT(XF[:, d, cs], PS[k][:, :], HG[:, l, 8 * j + d:8 * j + d + 1], XF[:, d, cs], ALU.mult, ALU.add,
                        [psr[k], rg["HG"], xr(d, hf)], [xr(d, hf)])

        def emit_out_proj(nm, src, src_r, gate_ap):
            for d in range(8):
                s = ws.use(("lh", nm, d))
                slot = RING[s]
                for hf in range(2):
                    cs = slice(hf * 512, (hf + 1) * 512)
                    k = 4 + y_cnt[0] % 2
                    y_cnt[0] += 1
                    for kc in range(8):
                        MM(PS[k][:, :], slot[:, kc * 128:(kc + 1) * 128], src[:, kc, cs], kc == 0, kc == 7,
                           [ring_r[s], src_r(kc, hf)], [psr[k]])
                    STT(XF[:, d, cs], PS[k][:, :], gate_ap[:, d:d + 1], XF[:, d, cs], ALU.mult, ALU.add,
                        [psr[k], rg["HG"], xr(d, hf)], [xr(d, hf)])

        EPSB = LAMV[:, 6:7]
        MEMSET("dve", EPSB, EPS, [rg["LAMV"]])

        xl_cnt = [0]
        VTB = ACTa[:, 0:8192].rearrange("p (b n) -> p b n", b=8)
        KNALL = ACTa[:, 8192:16384].rearrange("p (h n) -> p h n", h=8)
        QNALL = ACTa[:, 16384:20480].rearrange("p (h n) -> p h n", h=8)
        for T in range(NT1):
            first = T == 0
            for blk in range(8):
                while xl_cnt[0] < min(NT1 * 8, T * 8 + blk + 4):
                    g = xl_cnt[0]
                    DMA("sp", XIN[g % 4], xall[g * 128:(g + 1) * 128, :], rg["XIN%d" % (g % 4)], [], [rg["XIN%d" % (g % 4)]])
                    xl_cnt[0] += 1
                xi = XIN[blk % 4]
                xir = rg["XIN%d" % (blk % 4)]
                for c4 in range(2):
                    for cc in range(4):
                        c = c4 * 4 + cc
                        TR(PS[7][:, cc * 128:(cc + 1) * 128], xi[:, c * 128:(c + 1) * 128], IDF[:, :], [xir, rg["IDF"]], [psr[7]])
                    hf = blk // 4
                    CP("act", XF[:, c4 * 4:c4 * 4 + 4, blk * 128:(blk + 1) * 128], PS[7][:, :].rearrange("p (c t) -> p c t", c=4),
                       [psr[7]], [xr(c, hf) for c in range(c4 * 4, c4 * 4 + 4)])
            emit_ffn(0, 0, first)
            DMA("sp", X1[T], XF[:, :, 0:512], rg["X1st"], [xr(c, 0) for c in range(8)], [rg[("X1w", T)]])
            emit_norm(GM[:, 0, 8:16], HB, hr)
            units = []
            for qk in (1, 0):
                for h in range(H):
                    for hf in ((0, 1) if qk == 1 else (0,)):
                        units.append((qk, h, hf))
            ustate = {}

            def kq_a(u):
                qk, h, hf = units[u]
                col = qk * 8 + h
                if hf == 0:
                    s = ws.use(("lh", "qkv", col))
                    ustate["slot"] = s
                    if first:
                        for kc in range(8):
                            MM(PS[7][:, col:col + 1], RING[s][:, kc * 128:(kc + 1) * 128], SHB[:, 0, 8 + kc:9 + kc], kc == 0, kc == 7,
                               [ring_r[s], rg["SHB"]], [psr[7]])
                        CP("dve", CQK[:, col:col + 1], PS[7][:, col:col + 1], [psr[7]], [rg["CQK"]])
                s = ustate["slot"]
                slot = RING[s]
                cs = slice(hf * 512, (hf + 1) * 512)
                k = u % 2
                for kc in range(8):
                    MM(PS[2 * k][:, :], slot[:, kc * 128:(kc + 1) * 128], HB[:, kc, cs], kc == 0, kc == 7,
                       [ring_r[s], hr(kc, hf)], [psr[2 * k]])
                ACT(KF[k], PS[2 * k][:, :], AF.Identity, [psr[2 * k], rg["CQK"]], [rg["KF%d" % k]], bias=CQK[:, col:col + 1])
                ACT(GS[k], KF[k], AF.Square, [rg["KF%d" % k]], [rg["GS%d" % k]])

            def kq_b(u):
                qk, h, hf = units[u]
                k = u % 2
                MM(PS[2 * k + 1][:, :], BD64[:, :], GS[k], True, True, [rg["BD64"], rg["GS%d" % k]], [psr[2 * k + 1]])
                RSTD_ACT(RSTD[k], RS, rg["RS"], PS[2 * k + 1][:, :], 1.0 / 64, [psr[2 * k + 1]], [rg["RSTD%d" % k]])
                if qk == 1:
                    STT(KNALL[:, h, hf * 512:(hf + 1) * 512], KF[k], AVEC[:, qk:qk + 1], RSTD[k], ALU.mult, ALU.mult,
                        [rg["KF%d" % k], rg["AVEC"], rg["RSTD%d" % k]], [rg["KNALL%d" % hf]])
                else:
                    STT(QNALL[:, h, :], KF[k], AVEC[:, qk:qk + 1], RSTD[k], ALU.mult, ALU.mult,
                        [rg["KF%d" % k], rg["AVEC"], rg["RSTD%d" % k]], [rg["QNALL"]])

            kq_a(0)
            for u in range(len(units)):
                if u + 1 < len(units):
                    kq_a(u + 1)
                kq_b(u)
            for hf in range(2):
                pos0 = (4 * T) if hf == 0 else (NOWN + 4 * T)
                DMA("sp", KT[:, :, pos0 * 128:pos0 * 128 + 512].rearrange("h p n -> p h n"), KNALL[:, :, hf * 512:(hf + 1) * 512],
                    rg["KNALL%d" % hf], [rg["KNALL%d" % hf]] + [ar_(f, h2) for f in range(8, 16) for h2 in range(2)], [rg[("KTw", T, hf)]])
            DMA("sp", QT[:, :, T * 512:(T + 1) * 512].rearrange("h p n -> p h n"), QNALL[:, :, :],
                rg["QNALL"], [rg["QNALL"]] + [ar_(f, h2) for f in range(16, 20) for h2 in range(2)], [rg[("QTw", T)]])
            for ct in range(4):
                s = ws.use(("v", ct))
                slot3 = RING[s][:, 0:2048].rearrange("p (t k j) -> p t k j", t=2, k=8)
                if first:
                    for kc in range(8):
                        MM(PS[7][0:1, 256:512], SHB[:, 0, 8 + kc:9 + kc], slot3[:, :, kc, :], kc == 0, kc == 7,
                           [ring_r[s], rg["SHB"]], [psr[7]])
                    CP("dve", CVROW[0:1, ct * 256:(ct + 1) * 256], PS[7][0:1, 256:512], [psr[7]], [rg["BSROW"]])
                    MM(PS[7][:, 0:256], ONESF[0:1, :], CVROW[0:1, ct * 256:(ct + 1) * 256], True, True,
                       [rg["ONESF"], rg["BSROW"]], [psr[7]])
                    CP("dve", CVB[:, ct * 256:(ct + 1) * 256], PS[7][:, 0:256], [psr[7]], [rg["CVB"]])
                for blk in range(8):
                    hf = blk // 4
                    k = gu_cnt[0] % 4
                    gu_cnt[0] += 1
                    for kc in range(8):
                        MM(PS[k][:, 0:256], HB[:, kc, blk * 128:(blk + 1) * 128], slot3[:, :, kc, :], kc == 0, kc == 7,
                           [ring_r[s], hr(kc, hf)], [psr[k]])
                    TT(VTB[:, blk, ct * 256:(ct + 1) * 256], PS[k][:, 0:256], CVB[:, ct * 256:(ct + 1) * 256], ALU.add,
                       [psr[k], rg["CVB"]], [rg["VTB%d" % blk]])
            for blk in range(8):
                pos = (4 * T + blk) if blk < 4 else (NOWN + 4 * T + blk - 4)
                DMA("sp", VV[:, :, pos, :].rearrange("h p e -> p h e"), VTB[:, blk, :].rearrange("p (h e) -> p h e", h=8),
                    rg["VTB%d" % blk], [rg["VTB%d" % blk]] + [ar_(blk, h2) for h2 in range(2)], [rg[("VVw", T, blk)]])

        P.barrier()
        AR.reset()
        KTB = [AR.bf16(NB * 128) for _ in range(2)]
        VB = [AR.bf16(NB * 128).rearrange("p (n e) -> p n e", e=128) for _ in range(2)]
        QAB = [AR.bf16(NOWN * 128) for _ in range(2)]
        PT = [AR.bf16(512) for _ in range(6)]
        PSB = [AR.bf16(512) for _ in range(2)]
        BIAS = AR.f32(24 * 128).rearrange("p (n q) -> p n q", n=24)
        os_off = AR.off
        OS = [[AR.f32(512) for _ in range(4)] for _ in range(2)]
        HK = ARENA[:, os_off:os_off + 24 * 128]
        RSo = AR.f32(512)
        RSo2 = AR.f32(512)
        SQo = AR.bf16(512)
        AOB = [AR.bf16(512) for _ in range(2)]
        TAB33 = AR.f32(8)
        EH = AR.f32(768)
        GSB = AR.f32(768)

        DMA("sp", TAB33[0:32, :], relb, rg["TAB33"], [], [rg["TAB33"]])
        MEMSET("dve", TAB33[32:33, :], -1e30, [rg["TAB33"]])
        DMA("sp", EH[0:33, :], ehot, rg["EH"], [], [rg["EH"]])
        for hb in range(2):
            MM(PS[7][0:8, 0:384], TAB33[0:33, :], EH[0:33, hb * 384:(hb + 1) * 384], True, True, [rg["TAB33"], rg["EH"]], [psr[7]])
            CP("dve", GSB[0:8, hb * 384:(hb + 1) * 384], PS[7][0:8, 0:384], [psr[7]], [rg["GSB"]])
        DMA("sp", GD, GSB[0:8, :], rg["GSB"], [rg["GSB"]], [rg["GD"]])
        for sl in range(3):
            for h in range(H):
                n = sl * 8 + h
                DMA("sp", HK[:, n * 128:(n + 1) * 128], bass.AP(GD.tensor, h * 768 + sl * 256, [[1, 128], [1, 128]]),
                    rg["HK"], [rg["GD"]], [rg["HK"]])
        NB31 = AR.f32(8)
        TS(NB31, B31[:, :], -1.0, None, ALU.mult, None, [rg["B31"]], [rg["NB31"]])
        for n4 in range(6):
            MM(PS[7][:, :], JF[:, :], HK[:, n4 * 512:(n4 + 1) * 512], True, True, [rg["JF"], rg["HK"]], [psr[7]])
            for j4 in range(4):
                n = n4 * 4 + j4
                ACT(BIAS[:, n, :], PS[7][:, j4 * 128:(j4 + 1) * 128], AF.Exp, [psr[7], rg["NB31"]], [rg["BIAS"]],
                    bias=NB31[:, n % 8:n % 8 + 1])
        MEMSET("dve", QAB[0][64:128, :], 0.0, [rg["QA%d" % G] for G in range(NG)])
        MEMSET("dve", QAB[1][0:64, :], 0.0, [rg["QB%d" % G] for G in range(NG)])

        def ada_load(l, jg):
            bb = jg % 2
            sr = rg["PREPF%d" % bb]
            DMA("sp", PF[:, bb, :].rearrange("p (k j) -> p k j", k=8),
                ada_w[l, :, jg * 256:(jg + 1) * 256].rearrange("(k p) j -> p k j", p=128), sr, [], [sr])

        def ada_compute(l, jg):
            bb = jg % 2
            st = PF[:, bb, :]
            sr = rg["PREPF%d" % bb]
            for jj in range(2):
                col = jg * 2 + jj
                for kc in range(8):
                    MM(PS[7][:, col:col + 1], st[:, kc * 256 + jj * 128:kc * 256 + jj * 128 + 128], CACT[:, kc:kc + 1],
                       kc == 0, kc == 7, [sr, rg["CACT"]], [psr[7]])
            TT(MODV[:, l, jg * 2:jg * 2 + 2], PS[7][:, jg * 2:jg * 2 + 2], ADAB[:, l, jg * 2:jg * 2 + 2], ALU.add,
               [psr[7], rg["ADAB"]], [rg["MODV"]])

        def mod_derive(l):
            for j in range(3):
                STT(GM[:, l, 8 * j:8 * j + 8], MODV[:, l, (3 * j + 1) * 8:(3 * j + 2) * 8], 1.0, LNV[:, l, 8 * j:8 * j + 8],
                    ALU.add, ALU.mult, [rg["MODV"], rg["LNV"]], [rg["GM"]])
                CP("dve", SHB[:, l, 8 * j:8 * j + 8], MODV[:, l, 3 * j * 8:3 * j * 8 + 8], [rg["MODV"]], [rg["SHB"]])
                TS(HG[:, l, 8 * j:8 * j + 8], MODV[:, l, (3 * j + 2) * 8:(3 * j + 3) * 8], 1.0 if j == 1 else 0.5, None,
                   ALU.mult, None, [rg["MODV"]], [rg["HG"]])

        s_cnt = [0]
        pt_cnt = [0]
        hg_cnt = [0]
        SRING = (0, 1, 2, 7)
        psb_cnt = [0]
        NHG = H * NG
        deferred = []

        def load_kv(hh):
            bb = hh % 2
            DMA("sp", KTB[bb], KT[hh], rg["KTB%d" % bb], [rg["KT"]], [rg["KTB%d" % bb]])
            DMA("sp", VB[bb], VV[hh], rg["VB%d" % bb], [rg["VV"]], [rg["VB%d" % bb]])

        load_kv(0)
        for h in range(H):
            b = h % 2
            rk, rv, rqa, rqb = rg["KTB%d" % b], rg["VB%d" % b], rg["QA"], rg["QB"]
            if h + 1 < H:
                load_kv(h + 1)
            for G in range(NG):
                DMA("sp", QAB[0][0:64, G * 512:(G + 1) * 512], QT[h, 0:64, G * 512:(G + 1) * 512], rg["QA%d" % G], [rg["QT"]], [rg["QA%d" % G]])
                DMA("sp", QAB[1][64:128, G * 512:(G + 1) * 512], QT[h, 64:128, G * 512:(G + 1) * 512], rg["QB%d" % G], [rg["QT"]], [rg["QB%d" % G]])
            for G in range(NG):
                i0 = 4 * G
                kbs = []
                for jj in range(i0):
                    kbs.append((jj, 0, []))
                for jj in range(i0 - 1):
                    kbs.append((NOWN + jj, 0, []))
                for n in range(4):
                    kbs.append((i0 + n, n, [(2, n)]))
                for n in range(5):
                    jj = i0 - 1 + n
                    if jj < 0:
                        continue
                    segs = []
                    if n - 1 >= 0:
                        segs.append((1, n - 1))
                    if n <= 3:
                        segs.append((0, n))
                    kbs.append((NOWN + jj, max(0, n - 1), segs))
                far_l = [t for t in kbs if not t[2]]
                near_l = sorted([t for t in kbs if t[2]], key=lambda t: t[1])
                if far_l:
                    kbs = [far_l[0]]
                    fi = 1
                    for t in near_l:
                        kbs.append(t)
                        if fi < len(far_l):
                            kbs.append(far_l[fi])
                            fi += 1
                    kbs += far_l[fi:]
                else:
                    kbs = near_l
                items = [(c, kb) for c in range(2) for kb in kbs]
                nk = len(kbs)
                sbank = {}

                def qk(idx):
                    c, (pos, lo, segs) = items[idx]
                    sb_ = SRING[s_cnt[0] % 4]
                    s_cnt[0] += 1
                    sbank[idx] = sb_
                    MM(PS[sb_][:, lo * 128:512], KTB[b][:, pos * 128:(pos + 1) * 128],
                       QAB[c][:, G * 512 + lo * 128:(G + 1) * 512], True, True,
                       [rk, rg["QA%d" % G] if c == 0 else rg["QB%d" % G]], [psr[sb_]])

                den_state = {"held": None, "started": [False, False]}

                den_q = []

                def den_flush(keep):
                    while len(den_q) > keep:
                        c_, rhs_ap, lo_, rd_, last_ = den_q.pop(0)
                        st = den_state["started"]
                        MM(PS[5 + c_][:, lo_ * 128:512], ONESB[:, :], rhs_ap, not st[c_], last_, [rg["ONESB"]] + rd_, [psr[5 + c_]])
                        st[c_] = True

                def den_mm(c, rhs_ap, lo, rd, last):
                    den_q.append((c, rhs_ap, lo, rd, last))

                def pv(idx):
                    c, (pos, lo, segs) = items[idx]
                    sb_ = sbank[idx]
                    p = pt_cnt[0] % 6
                    pt_cnt[0] += 1
                    rp = rg["PT%d" % p]
                    ACT(PT[p][:, lo * 128:512], PS[sb_][:, lo * 128:512], AF.Exp, [psr[sb_]], [rp], scale=0.125)
                    for sl, m in segs:
                        TT(PT[p][:, m * 128:(m + 1) * 128], PT[p][:, m * 128:(m + 1) * 128], BIAS[:, sl * 8 + h, :], ALU.mult,
                           [rp, rg["BIAS"]], [rp])
                    ki = idx % nk
                    last = ki == nk - 1
                    MM(PS[3 + c][:, lo * 128:512], VB[b][:, pos, :], PT[p][:, lo * 128:512], ki == 0, last, [rv, rp], [psr[3 + c]])
                    held = den_state["held"]
                    if held is not None:
                        hp, hrp = held
                        q2 = psb_cnt[0] % 2
                        psb_cnt[0] += 1
                        rs2 = rg["PSUMB%d" % q2]
                        den_state["held"] = None
                        if lo == 0:
                            TT(PSB[q2], PT[hp], PT[p], ALU.add, [hrp, rp], [rs2])
                            den_mm(c, PSB[q2], 0, [rs2], last)
                            return
                        den_mm(c, PT[hp], 0, [hrp], False)
                    nxt_same = (not last) and lo == 0
                    if nxt_same:
                        den_state["held"] = (p, rp)
                    else:
                        den_mm(c, PT[p][:, lo * 128:512], lo, [rp], last)

                LOOK = 3
                for idx in range(min(LOOK, len(items))):
                    qk(idx)
                for idx in range(len(items)):
                    pv(idx)
                    den_flush(1)
                    if idx + LOOK < len(items):
                        qk(idx + LOOK)
                    while deferred and deferred[0][0] <= idx:
                        deferred.pop(0)[1]()
                den_flush(0)
                while deferred:
                    deferred.pop(0)[1]()
                ob = (h * NG + G) % 2
                o0, o1, d0, d1 = OS[ob]
                CP("dve", d0, PS[5][:, :], [psr[5]], [rg["OSd0%d" % ob]])
                CP("dve", d1, PS[6][:, :], [psr[6]], [rg["OSd1%d" % ob]])
                CP("dve", o0, PS[3][:, :], [psr[3]], [rg["OSo0%d" % ob]])
                CP("dve", o1, PS[4][:, :], [psr[4]], [rg["OSo1%d" % ob]])
                RECIP(d0, d0, [rg["OSd0%d" % ob]], [rg["OSd0%d" % ob]])
                RECIP(d1, d1, [rg["OSd1%d" % ob]], [rg["OSd1%d" % ob]])
                TT(o0, o0, d0, ALU.mult, [rg["OSo0%d" % ob], rg["OSd0%d" % ob]], [rg["OSo0%d" % ob]])
                TT(o1, o1, d1, ALU.mult, [rg["OSo1%d" % ob], rg["OSd1%d" % ob]], [rg["OSo1%d" % ob]])
                STT(o0, o1, NLAM, o0, ALU.mult, ALU.add, [rg["OSo0%d" % ob], rg["OSo1%d" % ob], rg["LAMV"]], [rg["OSo0%d" % ob]])

                def fin_a(ob=ob, o0=o0):
                    ACT(SQo, o0, AF.Square, [rg["OSo0%d" % ob]], [rg["SQo"]])

                def fin_b(ob=ob):
                    fb = SRING[s_cnt[0] % 4]
                    s_cnt[0] += 1
                    MM(PS[fb][:, :], ONESB[:, :], SQo, True, True, [rg["ONESB"], rg["SQo"]], [psr[fb]])
                    RSTD_ACT(RSo, RSo2, rg["RSo2"], PS[fb][:, :], 1.0 / 128, [psr[fb]], [rg["RSo"]])

                def fin_d(ob=ob, o0=o0, h=h, G=G):
                    STT(AOB[ob], o0, GSUB, RSo, ALU.mult, ALU.mult, [rg["OSo0%d" % ob], rg["LAMV"], rg["RSo"]], [rg["AOB%d" % ob]])
                    DMA("sp", AO[h, :, G * 512:(G + 1) * 512], AOB[ob], rg["AOB%d" % ob], [rg["AOB%d" % ob]], [rg[("AOw", h, G)]])

                deferred.extend([(14, fin_a), (19, fin_b), (24, fin_d)])
        while deferred:
            deferred.pop(0)[1]()

        P.barrier()
        UB = ACTa[:, 0:8192].rearrange("p (c t) -> p c t", c=8)
        VBf = ACTa[:, 8192:16384].rearrange("p (c t) -> p c t", c=8)
        VTK = ACTa[:, 16384:20480]
        yev = []
        for T2 in range(NT2):
            first = T2 == 0
            for hf in range(2):
                DMA("sp", XF[:, :, hf * 512:(hf + 1) * 512], X1[2 * T2 + hf], rg["XFld%d" % hf], [rg["X1"]], [xr(c, hf) for c in range(8)])
            for hf in range(2):
                DMA("sp", HB[:, :, hf * 512:(hf + 1) * 512], AO[:, :, T2 * 1024 + hf * 512:T2 * 1024 + (hf + 1) * 512].rearrange("h p n -> p h n"),
                    rg["HBld%d" % hf], [rg["AO"]], [hr(c, hf) for c in range(8)])
            emit_out_proj("wo", HB, hr, HG[:, 0, 8:16])
            emit_ffn(0, 1, first)
            emit_norm(LNV[:, 0, 24:32], XF, xr)
            emit_ffn(1, 0, first)
            emit_norm(GM[:, 1, 8:16], HB, hr)
            for f in range(16):
                s = ws.use(("lh", "win", f))
                slot = RING[s]
                if first:
                    for kc in range(8):
                        MM(PS[7][:, f:f + 1], slot[:, kc * 128:(kc + 1) * 128], SHB[:, 1, 8 + kc:9 + kc], kc == 0, kc == 7,
                           [ring_r[s], rg["SHB"]], [psr[7]])
                    CP("dve", CZ[:, f:f + 1], PS[7][:, f:f + 1], [psr[7]], [rg["CZ"]])
                for hf in range(2):
                    cs = slice(hf * 512, (hf + 1) * 512)
                    k = gu_cnt[0] % 4
                    gu_cnt[0] += 1
                    for kc in range(8):
                        MM(PS[k][:, :], slot[:, kc * 128:(kc + 1) * 128], HB[:, kc, cs], kc == 0, kc == 7, [ring_r[s], hr(kc, hf)], [psr[k]])
                    dst = UB[:, f, cs] if f < 8 else VBf[:, f - 8, cs]
                    ACT(dst, PS[k][:, :], AF.Gelu, [psr[k], rg["CZ"]], [rg["Z%d_%d" % (f, hf)]], bias=CZ[:, f:f + 1])
            for hf in range(2):
                cs = slice(hf * 512, (hf + 1) * 512)
                vrs = [rg["Z%d_%d" % (8 + c, hf)] for c in range(8)]
                for c in range(8):
                    MM(PS[6][:, :], ONESB[:, :], VBf[:, c, cs], c == 0, c == 7, [rg["ONESB"], vrs[c]], [psr[6]])
                for c in range(8):
                    STT(VC[:, c, :], PS[6][:, :], -1.0 / 1024, VBf[:, c, cs], ALU.mult, ALU.add, [psr[6], vrs[c]], [rg["VC"]])
                ACT(SQ[:, :, :], VC[:, :, :], AF.Square, [rg["VC"]], [rg["SQ"]])
                for c in range(8):
                    MM(PS[6][:, :], ONESB[:, :], SQ[:, c, :], c == 0, c == 7, [rg["ONESB"], rg["SQ"]], [psr[6]])
                RSTD_ACT(RSTD[hf], RS, rg["RS"], PS[6][:, :], 1.0 / 1024, [psr[6]], [rg["RSTD%d" % hf]])
                for c in range(8):
                    STT(VBf[:, c, cs], VC[:, c, :], SGV[:, c:c + 1], RSTD[hf], ALU.mult, ALU.mult,
                        [rg["VC"], rg["SGV"], rg["RSTD%d" % hf]], [vrs[c]])
                vtk = VTK.rearrange("p (n g c) -> p n g c", n=4, g=8)
                for n in range(4):
                    for g4 in range(2):
                        pbf = PS[7][:, 0:256].bitcast(BF16)
                        for gg in range(4):
                            g = g4 * 4 + gg
                            TR(pbf[:, gg * 128:(gg + 1) * 128], VBf[:, g, hf * 512 + n * 128:hf * 512 + (n + 1) * 128], IDB[:, :],
                               [vrs[g], rg["IDB"]], [psr[7]])
                        CP("act", vtk[:, n, g4 * 4:g4 * 4 + 4, :], pbf.rearrange("p (g c) -> p g c", g=4), [psr[7]], [rg["VTK"]])
                for g in range(8):
                    k = gu_cnt[0] % 4
                    gu_cnt[0] += 1
                    for n in range(4):
                        MM(PS[k][:, n * 128:(n + 1) * 128], vtk[:, n, g, :], WMT[:, g, :], True, True, [rg["VTK"], rg["WMT"]], [psr[k]])
                    TT(VC[:, g, :].rearrange("p (n t) -> p n t", n=4), PS[k][:, :].rearrange("p (n t) -> p n t", n=4),
                       B2[:, g, :].unsqueeze(1).to_broadcast([128, 4, 128]), ALU.add, [psr[k], rg["B2"]], [rg["VC"]])
                    TT(VBf[:, g, cs], VC[:, g, :], UB[:, g, cs], ALU.mult, [rg["VC"], rg["Z%d_%d" % (g, hf)]], [vrs[g]])
            emit_out_proj("wout", VBf, lambda c, hf: rg["Z%d_%d" % (8 + c, hf)], HG[:, 1, 8:16])
            emit_ffn(1, 1, first)
            emit_norm(LNV[:, 1, 24:32], XF, xr)
            for blk in range(8):
                yt = YT[blk % 4]
                ytr = rg["XIN%d" % (blk % 4)]
                hf = blk // 4
                for c4 in range(2):
                    k = gu_cnt[0] % 4
                    gu_cnt[0] += 1
                    for cc in range(4):
                        c = c4 * 4 + cc
                        TR(PS[k][:, cc * 128:(cc + 1) * 128], XF[:, c, blk * 128:(blk + 1) * 128], IDF[:, :], [xr(c, hf), rg["IDF"]], [psr[k]])
                    CP("act" if c4 == 0 else "dve", yt[:, c4 * 512:(c4 + 1) * 512], PS[k][:, :], [psr[k]], [ytr])
                row0 = T2 * 1024 + blk * 128
                yev.append(DMA("sp", y[row0:row0 + 128, :], yt, ytr, [ytr], [rg[("Yw", T2, blk)]]))
        P.wait("sp", yev)

    ws = WStream()
    emit_all(Prog(dry=True), ws)
    ws.collect = False
    P = Prog()
    emit_all(P, ws)
    assert ws.ptr == len(ws.keys)
    P.emit(nc)
    es.close()
    return nc


def _t5_bucket(rel):
    n = np.maximum(rel, 0)
    nf = np.maximum(n, 1).astype(np.float32)
    large = 16 + (np.log(nf / np.float32(16)) / np.float32(math.log(8.0)) * np.float32(16)).astype(np.int32)
    large = np.minimum(large, 31)
    return np.where(n < 16, n, large)


def host_inputs(inp, S, cores):
    NB = S // 128
    NT1 = S // 1024
    f32 = np.float32
    c = lambda a: np.ascontiguousarray(a, dtype=f32)
    common = {
        "ada_w": c(inp["ada_w"]),
        "ada_b2": c(inp["ada_b"].reshape(2, 72, 128).transpose(2, 0, 1)),
        "lnv": c(np.stack([inp["ln_ffn1"], inp["ln_mix"], inp["ln_ffn2"], inp["ln_out"]], 1).reshape(2, 4, 8, 128)
                 .transpose(3, 0, 1, 2).reshape(128, 2, 32)),
        "wg1": c(inp["ffn1_wg"]), "wu1": c(inp["ffn1_wu"]), "wd1": c(inp["ffn1_wd"]),
        "wg2": c(inp["ffn2_wg"]), "wu2": c(inp["ffn2_wu"]), "wd2": c(inp["ffn2_wd"]),
        "wqkv": c(inp["attn_w_qkv"][0]), "wo": c(inp["attn_w_o"][0]),
        "w_in": c(inp["sgu_w_in"][0]), "w_out": c(inp["sgu_w_out"][0]),
        "avec": c(np.stack([np.tile(inp["attn_q_norm"][0], 2), np.tile(inp["attn_k_norm"][0], 2), inp["attn_subln"][0],
                            np.zeros(128, f32)], 1)),
        "lrow": c(np.concatenate([inp["attn_lq1"][0], inp["attn_lk1"][0], inp["attn_lq2"][0], inp["attn_lk2"][0]])[None, :]),
        "sgv": c(np.concatenate([inp["sgu_ln_g"][0].reshape(8, 128).T, inp["sgu_ln_b"][0].reshape(8, 128).T], 1)),
        "w_s": c(inp["sgu_w_s"][0]),
        "bsrow": c(inp["sgu_b_s"][0].reshape(1, 1024)),
        "relb": c(inp["rel_bias"]),
        "ident": np.eye(128, dtype=f32),
        "tril": np.tril(np.ones((128, 128), f32)),
        "jflip": np.ascontiguousarray(np.eye(128, dtype=f32)[::-1]),
        "bd64": np.kron(np.eye(2, dtype=f32), np.ones((64, 64), f32)),
    }
    ehots = []
    for r in range(2):
        dists = (128, -128, 0) if r == 0 else (384, 128, 0)
        e = np.zeros((33, 768), f32)
        for sl, dd in enumerate(dists):
            j = np.arange(255)
            rel = j - 127 + dd
            bk = _t5_bucket(rel)
            for jj in range(255):
                if rel[jj] >= 0:
                    e[bk[jj], sl * 256 + jj] = 1.0
                else:
                    e[32, sl * 256 + jj] = 1.0
        ehots.append(e)
    maps = []
    perms = []
    for core in cores:
        b, r = core // 2, core % 2
        order = []
        for T in range(NT1):
            order += [2 * j + r for j in range(4 * T, 4 * T + 4)]
            order += [2 * j + 1 - r for j in range(4 * T, 4 * T + 4)]
        xb = inp["x"][b].reshape(NB, 128, D)
        m = dict(common)
        m["xall"] = c(xb[order].reshape(S, D))
        m["ccol"] = c(inp["c"][b].reshape(8, 128).T)
        m["ehot"] = ehots[r]
        maps.append(m)
        perms.append((b, r))
    return maps, perms


_NC_CACHE = {}


def kernel(**inputs):
    S = inputs["x"].shape[1]
    B = inputs["x"].shape[0]
    cores = list(range(2 * B))
    if S not in _NC_CACHE:
        _NC_CACHE[S] = build(S)
    nc = _NC_CACHE[S]
    maps, perms = host_inputs(inputs, S, cores)
    res = run_bass_kernel_spmd(nc, maps, core_ids=cores)
    NB = S // 128
    out = np.empty((B, NB, 128, D), np.float32)
    for (b, r), rr in zip(perms, res.results):
        out[b, r::2] = np.asarray(rr["y"]).reshape(NB // 2, 128, D)
    return out.reshape(B, S, D)
```
